# Optimizing a Trainium2 kernel written in Bass

```python
import jax, jax.numpy as jnp
from jax import lax
import numpy as np

D_MODEL = 1024
BATCH = 16
SEQ = 2048
DEPTH = 1
DEC_BATCH = 32
DEC_SEQ = 1
PAST_LEN = 16384
PAGE_SIZE = 128

N_MEM = 256
CHUNK = 128
Q_BLOCK = 128
ATT_HEADS = 8
ATT_HEAD_DIM = 64
ATT_WIDTH = ATT_HEADS * ATT_HEAD_DIM
SGU_HEADS = 8
SGU_HEAD_DIM = 64
SGU_WIDTH = SGU_HEADS * SGU_HEAD_DIM
MIX_WIDTH = ATT_WIDTH + SGU_WIDTH
O_Q = 0
O_K = ATT_WIDTH
O_V = 2 * ATT_WIDTH
O_F = 3 * ATT_WIDTH
O_U = O_F + ATT_HEADS
O_Z = O_U + SGU_WIDTH
N_IN = O_Z + SGU_WIDTH
MEM_HEADS = 4
MEM_HEAD_DIM = D_MODEL // MEM_HEADS
D_FF = 2816
CONV_W = 3
EPS = 1e-6

kernel_name = 'hybrid_fox_sgu_decoder_step'


def rmsnorm(x, g):
    xf = x.astype(jnp.float32)
    y = xf * lax.rsqrt(jnp.mean(xf * xf, axis=-1, keepdims=True) + EPS)
    return (y * g.astype(jnp.float32)).astype(x.dtype)


def mixer_inputs(xn, w_in, b_f, g_sgu_v):
    B, T, _ = xn.shape
    h = xn @ w_in
    q = h[..., O_Q:O_K].reshape(B, T, ATT_HEADS, ATT_HEAD_DIM)
    k = h[..., O_K:O_V].reshape(B, T, ATT_HEADS, ATT_HEAD_DIM)
    v = h[..., O_V:O_F].reshape(B, T, ATT_HEADS, ATT_HEAD_DIM)
    logf = jax.nn.log_sigmoid((h[..., O_F:O_U] + b_f).astype(jnp.float32))
    u = jax.nn.gelu(h[..., O_U:O_Z])
    z = rmsnorm(jax.nn.gelu(h[..., O_Z:]), g_sgu_v).reshape(B, T, SGU_HEADS, SGU_HEAD_DIM)
    return q, k, v, logf, u, z


def causal_chunk_weights(w_s):
    mask = jnp.tril(jnp.ones((CHUNK, CHUNK), dtype=bool))
    return jnp.where(mask, w_s, jnp.zeros((), w_s.dtype))


def sgu_prompt(u, z, w_s, b_s):
    B, T, H, hd = z.shape
    zc = z.reshape(B, T // CHUNK, CHUNK, H, hd)
    mixed = jnp.einsum('hts,bcshd->bcthd', causal_chunk_weights(w_s), zc) + b_s.T[None, None, :, :, None]
    return u * mixed.reshape(B, T, H * hd)


def sgu_sample(u, z, w_s, b_s):
    B, T, H, hd = z.shape
    wm = causal_chunk_weights(w_s)[:, :T, :T]
    mixed = jnp.einsum('hts,bshd->bthd', wm, z) + b_s[:, :T].T[None, :, :, None]
    return u * mixed.reshape(B, T, H * hd)


def fox_attend(q, k, v, cq, ck, qpos, kpos):
    s = jnp.einsum('bqhd,bkhd->bhqk', q, k).astype(jnp.float32) * (ATT_HEAD_DIM ** -0.5)
    s = s + (cq.astype(jnp.float32).transpose(0, 2, 1)[:, :, :, None]
             - ck.astype(jnp.float32).transpose(0, 2, 1)[:, :, None, :])
    s = jnp.where(kpos[None, :] <= qpos[:, None], s, -jnp.inf)
    p = jax.nn.softmax(s, axis=-1).astype(v.dtype)
    return jnp.einsum('bhqk,bkhd->bqhd', p, v)


def fox_prompt(q, k, v, logf):
    B, T, H, hd = q.shape
    c = jnp.cumsum(logf, axis=1)
    nb = T // Q_BLOCK
    qb = q.reshape(B, nb, Q_BLOCK, H, hd).swapaxes(0, 1)
    cqb = c.reshape(B, nb, Q_BLOCK, H).swapaxes(0, 1)
    starts = jnp.arange(nb, dtype=jnp.int32) * Q_BLOCK
    kpos = jnp.arange(T, dtype=jnp.int32)

    def one_block(args):
        qi, cqi, s0 = args
        return fox_attend(qi, k, v, cqi, c, s0 + jnp.arange(Q_BLOCK, dtype=jnp.int32), kpos)

    out = lax.map(one_block, (qb, cqb, starts))
    return out.swapaxes(0, 1).reshape(B, T, H * hd)


def fox_sample(q, k, v, logf, k_pool, v_pool, logf_pool, page_table):
    B, T, H, hd = q.shape
    past = page_table.shape[1] * PAGE_SIZE
    k_all = jnp.concatenate([k_pool[page_table].reshape(B, past, H, hd), k.astype(k_pool.dtype)], axis=1)
    v_all = jnp.concatenate([v_pool[page_table].reshape(B, past, H, hd), v.astype(v_pool.dtype)], axis=1)
    lf_all = jnp.concatenate([logf_pool[page_table].reshape(B, past, H).astype(jnp.float32), logf], axis=1)
    c = jnp.cumsum(lf_all, axis=1)
    qpos = past + jnp.arange(T, dtype=jnp.int32)
    kpos = jnp.arange(past + T, dtype=jnp.int32)
    out = fox_attend(q.astype(k_all.dtype), k_all, v_all, c[:, past:], c, qpos, kpos)
    return out.reshape(B, T, H * hd).astype(q.dtype)


def merge_heads(att, sgu, g_att_out, g_sgu_out, w_o):
    return jnp.concatenate([rmsnorm(att, g_att_out), rmsnorm(sgu, g_sgu_out)], axis=-1) @ w_o


def memory_kv(mem, g_mem, w_ck, w_cv):
    B, M, _ = mem.shape
    mn = rmsnorm(mem, g_mem)
    mk = (mn @ w_ck).reshape(B, M, MEM_HEADS, MEM_HEAD_DIM)
    mv = (mn @ w_cv).reshape(B, M, MEM_HEADS, MEM_HEAD_DIM)
    return mk, mv


def cross_attend(xn, mk, mv, w_cq, w_co):
    B, T, _ = xn.shape
    q = (xn @ w_cq).reshape(B, T, MEM_HEADS, MEM_HEAD_DIM).astype(mk.dtype)
    s = jnp.einsum('bqhd,bmhd->bhqm', q, mk).astype(jnp.float32) * (MEM_HEAD_DIM ** -0.5)
    p = jax.nn.softmax(s, axis=-1).astype(mv.dtype)
    o = jnp.einsum('bhqm,bmhd->bqhd', p, mv).reshape(B, T, MEM_HEADS * MEM_HEAD_DIM)
    return (o @ w_co).astype(xn.dtype)


def conv_ffn(xn, prev, w_up, conv_w, conv_b, w_down):
    T = xn.shape[1]
    h = xn @ w_up
    hp = jnp.concatenate([prev.astype(h.dtype), h], axis=1)
    hc = conv_b
    for i in range(CONV_W):
        hc = hc + conv_w[i] * hp[:, i:i + T]
    gate, val = jnp.split(hc, 2, axis=-1)
    return (jax.nn.silu(gate) * val) @ w_down, hp[:, T:]


def setup_inputs(seed: int = 0) -> dict:
    key = jax.random.key(seed)
    ks = jax.random.split(key, 40)
    n_pages = PAST_LEN // PAGE_SIZE
    n_used = DEC_BATCH * n_pages
    n_phys = n_used + n_used // 4

    def nrm(k, shape, scale):
        return jax.random.normal(k, shape, jnp.float32) * scale

    page_table = jax.random.permutation(ks[0], n_phys)[:n_used].reshape(DEC_BATCH, n_pages).astype(jnp.int32)
    return {
        'x_prompt': nrm(ks[1], (BATCH, SEQ, D_MODEL), 1.0),
        'x_sample': nrm(ks[2], (DEC_BATCH, DEC_SEQ, D_MODEL), 1.0),
        'cache_k': nrm(ks[3], (DEPTH, n_phys, PAGE_SIZE, ATT_HEADS, ATT_HEAD_DIM), 1.0),
        'cache_v': nrm(ks[4], (DEPTH, n_phys, PAGE_SIZE, ATT_HEADS, ATT_HEAD_DIM), 1.0),
        'cache_logf': jax.nn.log_sigmoid(2.0 + nrm(ks[5], (DEPTH, n_phys, PAGE_SIZE, ATT_HEADS), 1.0)),
        'cache_mem_k': nrm(ks[6], (DEPTH, DEC_BATCH, N_MEM, MEM_HEADS, MEM_HEAD_DIM), 1.0),
        'cache_mem_v': nrm(ks[7], (DEPTH, DEC_BATCH, N_MEM, MEM_HEADS, MEM_HEAD_DIM), 1.0),
        'state_conv': nrm(ks[8], (DEPTH, DEC_BATCH, CONV_W - 1, 2 * D_FF), 1.0),
        'page_table': page_table,
        'mem_prompt': nrm(ks[9], (BATCH, N_MEM, D_MODEL), 1.0),
        'g_mix': 1.0 + nrm(ks[10], (DEPTH, D_MODEL), 0.05),
        'w_in': nrm(ks[11], (DEPTH, D_MODEL, N_IN), D_MODEL ** -0.5),
        'b_f': 2.0 + nrm(ks[12], (DEPTH, ATT_HEADS), 0.1),
        'g_sgu_v': 1.0 + nrm(ks[13], (DEPTH, SGU_WIDTH), 0.05),
        'w_s': nrm(ks[14], (DEPTH, SGU_HEADS, CHUNK, CHUNK), CHUNK ** -0.5),
        'b_s': 1.0 + nrm(ks[15], (DEPTH, SGU_HEADS, CHUNK), 0.01),
        'g_att_out': 1.0 + nrm(ks[16], (DEPTH, ATT_WIDTH), 0.05),
        'g_sgu_out': 1.0 + nrm(ks[17], (DEPTH, SGU_WIDTH), 0.05),
        'w_o': nrm(ks[18], (DEPTH, MIX_WIDTH, D_MODEL), MIX_WIDTH ** -0.5),
        'g_cross': 1.0 + nrm(ks[19], (DEPTH, D_MODEL), 0.05),
        'g_mem': 1.0 + nrm(ks[20], (DEPTH, D_MODEL), 0.05),
        'w_cq': nrm(ks[21], (DEPTH, D_MODEL, D_MODEL), D_MODEL ** -0.5),
        'w_ck': nrm(ks[22], (DEPTH, D_MODEL, D_MODEL), D_MODEL ** -0.5),
        'w_cv': nrm(ks[23], (DEPTH, D_MODEL, D_MODEL), D_MODEL ** -0.5),
        'w_co': nrm(ks[24], (DEPTH, D_MODEL, D_MODEL), D_MODEL ** -0.5),
        'g_ffn': 1.0 + nrm(ks[25], (DEPTH, D_MODEL), 0.05),
        'w_up': nrm(ks[26], (DEPTH, D_MODEL, 2 * D_FF), D_MODEL ** -0.5),
        'conv_w': nrm(ks[27], (DEPTH, CONV_W, 2 * D_FF), CONV_W ** -0.5),
        'conv_b': nrm(ks[28], (DEPTH, 2 * D_FF), 0.01),
        'w_down': nrm(ks[29], (DEPTH, D_FF, D_MODEL), D_FF ** -0.5),
        'g_final': 1.0 + nrm(ks[30], (D_MODEL,), 0.05),
    }


def reference(x_prompt, x_sample, cache_k, cache_v, cache_logf, cache_mem_k, cache_mem_v, state_conv,
              page_table, mem_prompt, g_mix, w_in, b_f, g_sgu_v, w_s, b_s, g_att_out, g_sgu_out, w_o,
              g_cross, g_mem, w_cq, w_ck, w_cv, w_co, g_ffn, w_up, conv_w, conv_b, w_down, g_final):
    xp = x_prompt
    xs = x_sample
    bp = xp.shape[0]
    kp_l, vp_l, lfp_l, mkp_l, mvp_l, cvp_l = [], [], [], [], [], []
    ks_l, vs_l, lfs_l, zs_l, cvs_l = [], [], [], [], []
    for l in range(DEPTH):
        q, k, v, logf, u, z = mixer_inputs(rmsnorm(xp, g_mix[l]), w_in[l], b_f[l], g_sgu_v[l])
        xp = xp + merge_heads(fox_prompt(q, k, v, logf), sgu_prompt(u, z, w_s[l], b_s[l]),
                              g_att_out[l], g_sgu_out[l], w_o[l])
        mk, mv = memory_kv(mem_prompt, g_mem[l], w_ck[l], w_cv[l])
        xp = xp + cross_attend(rmsnorm(xp, g_cross[l]), mk, mv, w_cq[l], w_co[l])
        f, conv_p = conv_ffn(rmsnorm(xp, g_ffn[l]), jnp.zeros((bp, CONV_W - 1, 2 * D_FF), xp.dtype),
                             w_up[l], conv_w[l], conv_b[l], w_down[l])
        xp = xp + f
        kp_l.append(k); vp_l.append(v); lfp_l.append(logf)
        mkp_l.append(mk); mvp_l.append(mv); cvp_l.append(conv_p)

        q, k, v, logf, u, z = mixer_inputs(rmsnorm(xs, g_mix[l]), w_in[l], b_f[l], g_sgu_v[l])
        att = fox_sample(q, k, v, logf, cache_k[l], cache_v[l], cache_logf[l], page_table)
        xs = xs + merge_heads(att, sgu_sample(u, z, w_s[l], b_s[l]), g_att_out[l], g_sgu_out[l], w_o[l])
        xs = xs + cross_attend(rmsnorm(xs, g_cross[l]), cache_mem_k[l], cache_mem_v[l], w_cq[l], w_co[l])
        f, conv_s = conv_ffn(rmsnorm(xs, g_ffn[l]), state_conv[l], w_up[l], conv_w[l], conv_b[l], w_down[l])
        xs = xs + f
        ks_l.append(k); vs_l.append(v); lfs_l.append(logf); zs_l.append(z); cvs_l.append(conv_s)

    y_prompt = rmsnorm(xp, g_final)
    y_sample = rmsnorm(xs, g_final)
    k_prompt = jnp.stack(kp_l)
    v_prompt = jnp.stack(vp_l)
    logf_prompt = jnp.stack(lfp_l)
    mem_k_prompt = jnp.stack(mkp_l)
    mem_v_prompt = jnp.stack(mvp_l)
    conv_prompt = jnp.stack(cvp_l)
    k_sample = jnp.stack(ks_l)
    v_sample = jnp.stack(vs_l)
    logf_sample = jnp.stack(lfs_l)
    chunk_v_sample = jnp.stack(zs_l)
    conv_sample = jnp.stack(cvs_l)
    return (y_prompt, y_sample, k_prompt, v_prompt, logf_prompt, mem_k_prompt, mem_v_prompt, conv_prompt,
            k_sample, v_sample, logf_sample, chunk_v_sample, conv_sample)
```

```python
import contextlib
import numpy as np
import concourse.bass as bass
import concourse.mybir as mybir
from concourse.bass_utils import run_bass_kernel_spmd

F32 = mybir.dt.float32
BF16 = mybir.dt.bfloat16
I32 = mybir.dt.int32
ALU = mybir.AluOpType
AF = mybir.ActivationFunctionType
AX = mybir.AxisListType

D = 1024
NIN = 2568
DFF = 2816
F2 = 5632
NMEM = 256
EPS = 1e-6
GC0 = 1.5957691216057308
GC1 = 0.07135481627260025


class Buf:
    __slots__ = ("w", "r", "excl")

    def __init__(self, excl=False):
        self.w = {}
        self.r = {}
        self.excl = excl


class Eng:
    def __init__(self, name, eng, sem, is_pe=False):
        self.name = name
        self.eng = eng
        self.sem = sem
        self.cnt = 0
        self.waited = {}
        self.is_pe = is_pe

    def wait(self, key, sem, val, owner):
        if owner is self and self.is_pe:
            return
        if self.waited.get(key, 0) >= val:
            return
        self.eng.wait_ge(sem, val)
        self.waited[key] = val


class Ctx:
    def __init__(self, nc, es):
        self.nc = nc
        self.es = es
        self.cur = es
        self.nm = 0
        self.dsem = []
        self.dsem_i = 0

    def name(self, p):
        self.nm += 1
        return f"{p}{self.nm}"

    def sb(self, shape, dt, p="t"):
        return self.cur.enter_context(self.nc.sbuf_tensor(self.name(p), list(shape), dt))

    @contextlib.contextmanager
    def scope(self):
        prev = self.cur
        st = contextlib.ExitStack()
        self.cur = st
        try:
            yield
            self.full_barrier()
        finally:
            self.cur = prev
            st.close()

    def full_barrier(self):
        engs = (self.PE, self.ACT, self.DVE, self.POOL, self.SP)
        for E in engs:
            for ds in self.dsem:
                if ds[1] > 0:
                    E.wait(id(ds[0]), ds[0], ds[1], None)
            for O in engs:
                if O is not E and O.cnt > 0:
                    E.wait(id(O.sem), O.sem, O.cnt, O)

    def ps(self, shape, dt, p="p"):
        return self.es.enter_context(self.nc.psum_tensor(self.name(p), list(shape), dt))

    def sem(self, p="s"):
        return self.es.enter_context(self.nc.semaphore(self.name(p)))

    def setup(self):
        nc = self.nc
        self.PE = Eng("pe", nc.tensor, self.sem("pe"), True)
        self.ACT = Eng("act", nc.scalar, self.sem("act"))
        self.DVE = Eng("dve", nc.vector, self.sem("dve"))
        self.POOL = Eng("pool", nc.gpsimd, self.sem("pool"))
        self.SP = Eng("sp", nc.sync, self.sem("sp"))
        for i in range(40):
            self.dsem.append([self.sem("d"), 0])

    def _waits(self, E, reads, writes):
        for b in reads:
            for k, (s, v, o) in b.w.items():
                E.wait(k, s, v, o)
            if b.excl:
                for k, (s, v, o) in b.r.items():
                    if o is not E:
                        E.wait(k, s, v, o)
        for b in writes:
            for k, (s, v, o) in b.w.items():
                E.wait(k, s, v, o)
            for k, (s, v, o) in b.r.items():
                E.wait(k, s, v, o)

    def _mark(self, tok, reads, writes, merge=False):
        k = id(tok[0])
        for b in reads:
            b.r[k] = tok
        for b in writes:
            if merge:
                b.w[k] = tok
            else:
                b.w = {k: tok}
            b.r = {}

    def op(self, E, fn, reads=(), writes=()):
        self._waits(E, reads, writes)
        ins = fn(E.eng)
        E.cnt += 1
        ins.then_inc(E.sem, 1)
        self._mark((E.sem, E.cnt, E), reads, writes)

    def dma(self, Q, out, in_, reads=(), writes=(), indirect=None):
        self._waits(Q, reads, writes)
        ds = self.dsem[self.dsem_i]
        self.dsem_i = (self.dsem_i + 1) % len(self.dsem)
        Q.wait(id(ds[0]), ds[0], ds[1], None)
        if indirect is not None:
            ins = Q.eng.indirect_dma_start(out=out, out_offset=None, in_=in_,
                                           in_offset=bass.IndirectOffsetOnAxis(ap=indirect, axis=0))
        else:
            ins = Q.eng.dma_start(out=out, in_=in_)
        ds[1] += 16
        ins.then_inc(ds[0], 16)
        self._mark((ds[0], ds[1], None), reads, writes, merge=True)

    def finish(self):
        for ds in self.dsem:
            if ds[1] > 0:
                self.SP.wait(id(ds[0]), ds[0], ds[1], None)
        for E in (self.PE, self.ACT, self.DVE, self.POOL):
            if E.cnt > 0:
                self.SP.wait(id(E.sem), E.sem, E.cnt, E)


class Rot:
    def __init__(self, c, n, shape, dt, psum=False, p="r"):
        self.slots = []
        for i in range(n):
            t = c.ps(shape, dt, p) if psum else c.sb(shape, dt, p)
            self.slots.append((t, Buf(excl=psum)))
        self.i = 0

    def next(self):
        s = self.slots[self.i]
        self.i = (self.i + 1) % len(self.slots)
        return s


def build_main(NB, T, NSO):
    NT = T // 128
    NG = NT // 2
    nc = bass.Bass("TRN2", target_bir_lowering=False)

    def din(name, shape, dt=F32):
        return nc.dram_tensor(name, list(shape), dt, kind="ExternalInput").ap()

    def dout(name, shape, dt=F32):
        return nc.dram_tensor(name, list(shape), dt, kind="ExternalOutput").ap()

    def dscr(name, shape, dt=F32):
        return nc.dram_tensor(name, list(shape), dt, kind="Internal").ap()

    xp = din("xp", [NB * T, D])
    memp = din("memp", [NB * NMEM, D])
    w_in = din("w_in", [D, NIN])
    w_o = din("w_o", [D, D])
    w_cq = din("w_cq", [D, D])
    w_ck = din("w_ck", [D, D])
    w_cv = din("w_cv", [D, D])
    w_co = din("w_co", [D, D])
    w_up = din("w_up", [D, F2])
    w_down = din("w_down", [DFF, D])
    gvec = din("gvec", [6, D])
    g_sgu_v = din("g_sgu_v", [1, 512])
    b_f = din("b_f", [1, 8])
    w_sT = din("w_sT", [8, 128, 128])
    b_sT = din("b_sT", [128, 8])
    cwT = din("cwT", [128, 44, 4])
    ident_d = din("ident", [128, 128])
    mask_d = din("mask_le", [128, 128])
    xso = din("xso", [NSO, D])
    atto = din("atto", [NSO, 512])
    cmk = din("cmk", [NSO * NMEM, D])
    cmv = din("cmv", [NSO * NMEM, D])
    stc = din("stc", [NSO * 2, F2])
    sel4 = din("sel4", [NSO, NSO * 128])
    ws00x = din("ws00x", [1, 512])
    bs0x = din("bs0x", [1, 512])
    cwrow = din("cwrow", [4, F2])
    ys_o = dout("ys", [NSO, D])
    zs_o = dout("zs", [NSO, 512])
    cvs_o = dout("cvs", [NSO * 2, F2])
    x2s_s = dscr("x2s_s", [NSO, D])
    oc_s = dscr("oc_s", [NSO, D])

    y_o = dout("y", [NB * T, D])
    k_o = dout("k", [NB * T, 512])
    v_o = dout("v", [NB * T, 512])
    lf_o = dout("lf", [NB * T, 8])
    mk_o = dout("mk", [NB * NMEM, D])
    mv_o = dout("mv", [NB * NMEM, D])
    cv_o = dout("cv", [NB * 2, F2])

    qT_s = dscr("qT_s", [NB * 8 * 64, T], BF16)
    kT_s = dscr("kT_s", [NB * 8 * 64, T], BF16)
    va_s = dscr("va_s", [NB * T, 8 * 65], BF16)
    lf_s = dscr("lf_s", [NB * T, 8])
    sgu_s = dscr("sgu_s", [NB * T, 512], BF16)
    att_s = dscr("att_s2", [NB * T, 512])
    x2_s = dscr("x2_s", [NB * T, D])

    with contextlib.ExitStack() as es:
        c = Ctx(nc, es)
        c.setup()
        PE, ACT, DVE, POOL, SP = c.PE, c.ACT, c.DVE, c.POOL, c.SP
        block = es.enter_context(nc.Block())

        def const(shape, dt, src_ap, q=None):
            t = c.sb(shape, dt, "c")
            b = Buf()
            c.dma(q or SP, t[:], src_ap, writes=[b])
            return t, b

        ident_f, ident_fb = const([128, 128], F32, ident_d)
        mask_f, mask_fb = const([128, 128], F32, mask_d)
        ident = c.sb([128, 128], BF16, "c"); identb = Buf()
        c.op(DVE, lambda e: e.tensor_copy(out=ident[:], in_=ident_f[:]), [ident_fb], [identb])
        mask_b = c.sb([128, 128], BF16, "c"); mask_bb = Buf()
        c.op(DVE, lambda e: e.tensor_copy(out=mask_b[:], in_=mask_f[:]), [mask_fb], [mask_bb])
        ones_f = c.sb([128, 128], F32, "c"); ones_fb = Buf()
        c.op(DVE, lambda e: e.memset(ones_f[:], 1.0), [], [ones_fb])
        cw, cwb = const([128, 44, 4], F32, cwT)
        pA = Rot(c, 4, [128, 512], F32, psum=True, p="pa")
        pT = Rot(c, 2, [128, 1024], BF16, psum=True, p="pt")
        pS = Rot(c, 2, [128, 512], F32, psum=True, p="ps")

        class RotShared:
            def __init__(self, slots):
                self.slots = slots
                self.i = 0

            def next(self):
                sl = self.slots[self.i]
                self.i = (self.i + 1) % len(self.slots)
                return sl

        pX = RotShared(pA.slots + pS.slots)
        pmm = [pA]

        wst = {}
        wregb = Buf()

        def alloc_w(n):
            wst["w"] = c.sb([128, n], BF16, "w")

        def load_w(dst_off, w_ap, rows, cols, q=POOL):
            wreg = wst["w"]
            kcn = rows // 128
            view = wreg[:, dst_off:dst_off + kcn * cols].rearrange("p (k n) -> p k n", k=kcn)
            for k in range(kcn):
                for c0 in range(0, cols, 1024):
                    c1 = min(cols, c0 + 1024)
                    c.dma(q, view[:, k, c0:c1], w_ap[k * 128:(k + 1) * 128, c0:c1], writes=[wregb])
            return view

        junk = Rot(c, 2, [128, D], BF16, p="j")
        st1 = Rot(c, 6, [128, 1], F32, p="st")

        def rmsnorm(x_ap, xb, P, width, out_ap, outb, g_ap=None, gb=None):
            jt, jb = junk.next()
            ss, ssb = st1.next()
            c.op(DVE, lambda e: e.memset(ss[:P, :], 0.0), [], [ssb])
            c.op(ACT, lambda e: e.activation(out=jt[:P, :width], in_=x_ap, func=AF.Square, accum_out=ss[:P, :]),
                 [xb, ssb], [jb, ssb])
            rs, rsb = st1.next()
            c.op(ACT, lambda e: e.activation(out=rs[:P, :], in_=ss[:P, :], func=AF.Sqrt, scale=1.0 / width, bias=EPS),
                 [ssb], [rsb])
            rd, rdb = st1.next()
            c.op(DVE, lambda e: e.reciprocal(out=rd[:P, :], in_=rs[:P, :]), [rsb], [rdb])
            if g_ap is None:
                c.op(DVE, lambda e: e.tensor_scalar(out=out_ap, in0=x_ap, scalar1=rd[:P, :], scalar2=None, op0=ALU.mult),
                     [xb, rdb], [outb])
            else:
                c.op(DVE, lambda e: e.scalar_tensor_tensor(out=out_ap, in0=x_ap, scalar=rd[:P, :], in1=g_ap,
                                                           op0=ALU.mult, op1=ALU.mult),
                     [xb, rdb, gb], [outb])

        xT_rot = Rot(c, 2, [128, 8, 128], BF16, p="xT")

        def transpose_tile(src, srcb, P, nch):
            pt, ptb = pT.next()
            for ch in range(nch):
                c.op(PE, lambda e, ch=ch: e.transpose(out=pt[:, ch * 128:ch * 128 + P], in_=src[:P, ch * 128:(ch + 1) * 128],
                                                      identity=ident[:P, :P]),
                     [srcb, identb], [ptb])
            xt, xtb = xT_rot.next()
            c.op(DVE, lambda e: e.tensor_copy(out=xt[:, :nch, :P],
                                              in_=pt[:, :nch * 128].rearrange("p (c t) -> p c t", c=nch)[:, :, :P]),
                 [ptb], [xtb])
            return xt, xtb

        def mm_tok(xt, xtb, P, wview, nk, c0, ncols):
            ps, psb = pmm[0].next()
            for k in range(nk):
                c.op(PE, lambda e, k=k: e.matmul(ps[:P, :ncols], lhsT=xt[:, k, :P], rhs=wview[:, k, c0:c0 + ncols],
                                                 start=(k == 0), stop=(k == nk - 1)),
                     [xtb, wregb], [psb])
            return ps, psb

        def gelu_from_psum(ps, psb, P, n, out_ap, outb, ta, tab, tb_, tbb):
            c.op(ACT, lambda e: e.activation(out=ta[:P, :n], in_=ps[:P, :n], func=AF.Square), [psb], [tab])
            c.op(DVE, lambda e: e.tensor_scalar(out=ta[:P, :n], in0=ta[:P, :n], scalar1=GC1, scalar2=GC0,
                                                op0=ALU.mult, op1=ALU.add), [tab], [tab])
            c.op(DVE, lambda e: e.tensor_tensor(out=tb_[:P, :n], in0=ta[:P, :n], in1=ps[:P, :n], op=ALU.mult),
                 [tab, psb], [tbb])
            c.op(ACT, lambda e: e.activation(out=tb_[:P, :n], in_=tb_[:P, :n], func=AF.Sigmoid), [tbb], [tbb])
            c.op(DVE, lambda e: e.tensor_tensor(out=out_ap, in0=tb_[:P, :n], in1=ps[:P, :n], op=ALU.mult),
                 [tbb, psb], [outb])

        xin = Rot(c, 2, [128, D], F32, p="xin")
        xnb = Rot(c, 2, [128, D], BF16, p="xnb")
        rl = Rot(c, 4, [128, 1], F32, p="rl")
        with c.scope():
            grep, grepb = const([128, 6, D], F32, gvec.rearrange("(o g) d -> o g d", o=1).broadcast_to([128, 6, D]))
            gsv, gsvb = const([128, 512], F32, g_sgu_v.broadcast_to([128, 512]))
            bfr, bfrb = const([128, 8], F32, b_f.broadcast_to([128, 8]))
            bsT, bsTb = const([128, 8], F32, b_sT)
            wsT_f, wsT_fb = const([128, 8, 128], F32, w_sT.rearrange("h s t -> s h t"))
            wsT = c.sb([128, 8, 128], BF16, "c"); wsTb = Buf()
            for h in range(8):
                c.op(DVE, lambda e, h=h: e.tensor_tensor(out=wsT[:, h, :], in0=wsT_f[:, h, :], in1=mask_f[:], op=ALU.mult),
                     [wsT_fb, mask_fb], [wsTb])

            mkT = c.sb([128, NB, 8, NMEM], BF16, "mkT"); mkTb = Buf()
            mva = c.sb([128, NB, 2, 4 * 257], BF16, "mva"); mvab = Buf()
            c.op(DVE, lambda e: e.memset(mva[:], 1.0), [], [mvab])
            osb = Rot(c, 3, [128, 512], F32, p="osb")
            mgs = c.sb([NSO, D], BF16, "mgs"); mgsb = Buf()
            with c.scope():
                alloc_w(16384)
                wck = load_w(0, w_ck, D, D)
                wcv = load_w(8 * D, w_cv, D, D)
                mnT = c.sb([128, 8, NMEM], BF16, "mnT"); mnTb = Buf()
                for b in range(NB):
                    for mt in range(2):
                        r0 = b * NMEM + mt * 128
                        xt_, xb_ = xin.next()
                        c.dma(SP, xt_[:], memp[r0:r0 + 128, :], writes=[xb_])
                        xn_, xnb_ = xnb.next()
                        rmsnorm(xt_[:], xb_, 128, D, xn_[:], xnb_, grep[:, 2, :], grepb)
                        xT_, xTb_ = transpose_tile(xn_, xnb_, 128, 8)
                        c.op(DVE, lambda e, mt=mt: e.tensor_copy(out=mnT[:, :, mt * 128:(mt + 1) * 128], in_=xT_[:, :, :]),
                             [xTb_], [mnTb])
                        for (wv, dst, isv) in ((wck, mk_o, False), (wcv, mv_o, True)):
                            for cc in range(2):
                                ps, psb = mm_tok(xT_, xTb_, 128, wv, 8, cc * 512, 512)
                                o_, ob_ = osb.next()
                                c.op(ACT, lambda e: e.copy(out=o_[:], in_=ps[:, :]), [psb], [ob_])
                                c.dma(SP, dst[r0:r0 + 128, cc * 512:(cc + 1) * 512], o_[:], reads=[ob_])
                                if isv:
                                    for hh in range(2):
                                        m = cc * 2 + hh
                                        c.op(DVE, lambda e, m=m, hh=hh: e.tensor_copy(
                                            out=mva[:, b, mt, m * 257:m * 257 + 256], in_=o_[:, hh * 256:(hh + 1) * 256]),
                                            [ob_], [mvab])
                    for blk in range(8):
                        ps, psb = pA.next()
                        for k in range(8):
                            c.op(PE, lambda e, k=k, blk=blk: e.matmul(ps[:, :NMEM], lhsT=wck[:, k, blk * 128:(blk + 1) * 128],
                                                                      rhs=mnT[:, k, :], start=(k == 0), stop=(k == 7)),
                                 [wregb, mnTb], [psb])
                        c.op(ACT, lambda e, blk=blk: e.copy(out=mkT[:, b, blk, :], in_=ps[:, :NMEM]), [psb], [mkTb])

            with c.scope():
                alloc_w(20544)
                win = load_w(0, w_in, D, NIN)
                ga = Rot(c, 2, [128, 512], F32, p="ga")
                gb_ = Rot(c, 2, [128, 512], F32, p="gb")
                ubuf = Rot(c, 2, [128, 512], F32, p="u")
                zbuf = Rot(c, 2, [128, 512], F32, p="z")
                zbb = Rot(c, 2, [128, 512], BF16, p="zb")
                vab = Rot(c, 2, [128, 8, 65], BF16, p="va")
                for s_ in vab.slots:
                    c.op(DVE, lambda e, s_=s_: e.memset(s_[0][:], 1.0), [], [s_[1]])
                qkb = Rot(c, 3, [128, 128], BF16, p="qk")
                sm8 = Rot(c, 4, [128, 8], F32, p="s8")

                pmm[0] = pX
                def stage1_tile(src_ap, P, r0, prompt=True, b=0, tcol=0):
                    xt_, xb_ = xin.next()
                    c.dma(SP, xt_[:P, :], src_ap, writes=[xb_])
                    xn_, xnb_ = xnb.next()
                    rmsnorm(xt_[:P, :], xb_, P, D, xn_[:P, :], xnb_, grep[:P, 0, :], grepb)
                    xT_, xTb_ = transpose_tile(xn_, xnb_, P, 8)
                    for blk in range(8):
                        ps, psb = pX.next()
                        for k in range(8):
                            c.op(PE, lambda e, k=k, blk=blk: e.matmul(ps[:, :P], lhsT=win[:, k, blk * 128:(blk + 1) * 128],
                                                                      rhs=xT_[:, k, :P], start=(k == 0), stop=(k == 7)),
                                 [wregb, xTb_], [psb])
                        o_, ob_ = qkb.next()
                        sc = 0.125 if blk < 4 else 1.0
                        c.op(ACT, lambda e, sc=sc: e.activation(out=o_[:, :P], in_=ps[:, :P], func=AF.Copy, scale=sc), [psb], [ob_])
                        dst = qT_s if blk < 4 else kT_s
                        pr = blk % 4
                        c.dma(POOL, dst[b * 512 + pr * 128:b * 512 + (pr + 1) * 128, tcol:tcol + P], o_[:, :P], reads=[ob_])
                    for (c0, dsto, isv) in ((512, k_o, False), (1024, v_o, True)):
                        ps, psb = mm_tok(xT_, xTb_, P, win, 8, c0, 512)
                        o_, ob_ = osb.next()
                        c.op(ACT, lambda e: e.copy(out=o_[:P, :], in_=ps[:P, :]), [psb], [ob_])
                        c.dma(SP, dsto[r0:r0 + P, :], o_[:P, :], reads=[ob_])
                        if isv:
                            va_, vab_ = vab.next()
                            c.op(DVE, lambda e: e.tensor_copy(out=va_[:P, :, 0:64], in_=o_[:P, :].rearrange("p (h d) -> p h d", h=8)),
                                 [ob_], [vab_])
                            c.dma(POOL, va_s[r0:r0 + P, :], va_[:P, :, :].rearrange("p h d -> p (h d)"), reads=[vab_])
                    ps, psb = mm_tok(xT_, xTb_, P, win, 8, 1536, 8)
                    f1, f1b = sm8.next()
                    c.op(DVE, lambda e: e.tensor_tensor(out=f1[:P, :], in0=ps[:P, :8], in1=bfr[:P, :], op=ALU.add), [psb, bfrb], [f1b])
                    c.op(ACT, lambda e: e.activation(out=f1[:P, :], in_=f1[:P, :], func=AF.Exp, scale=-1.0), [f1b], [f1b])
                    c.op(ACT, lambda e: e.activation(out=f1[:P, :], in_=f1[:P, :], func=AF.Ln, bias=1.0), [f1b], [f1b])
                    f2, f2b = sm8.next()
                    c.op(DVE, lambda e: e.tensor_scalar(out=f2[:P, :], in0=f1[:P, :], scalar1=-1.0, scalar2=None, op0=ALU.mult), [f1b], [f2b])
                    c.dma(SP, lf_o[r0:r0 + P, :], f2[:P, :], reads=[f2b])
                    c.dma(SP, lf_s[r0:r0 + P, :], f2[:P, :], reads=[f2b])
                    ta, tab = ga.next(); tb_, tbb = gb_.next()
                    ps, psb = mm_tok(xT_, xTb_, P, win, 8, 1544, 512)
                    u_, ub_ = ubuf.next()
                    gelu_from_psum(ps, psb, P, 512, u_[:P, :], ub_, ta, tab, tb_, tbb)
                    ta, tab = ga.next(); tb_, tbb = gb_.next()
                    ps, psb = mm_tok(xT_, xTb_, P, win, 8, 2056, 512)
                    z_, zb_ = zbuf.next()
                    gelu_from_psum(ps, psb, P, 512, z_[:P, :], zb_, ta, tab, tb_, tbb)
                    zn_, znb_ = zbb.next()
                    rmsnorm(z_[:P, :], zb_, P, 512, zn_[:P, :], znb_, gsv[:P, :], gsvb)
                    ps, psb = pX.next()
                    for h in range(8):
                        c.op(PE, lambda e, h=h: e.matmul(ps[:P, h * 64:(h + 1) * 64], lhsT=wsT[:, h, :], rhs=zn_[:, h * 64:(h + 1) * 64],
                                                         start=True, stop=True), [wsTb, znb_], [psb])
                    mx, mxb = osb.next()
                    for h in range(8):
                        c.op(DVE, lambda e, h=h: e.scalar_tensor_tensor(out=mx[:P, h * 64:(h + 1) * 64], in0=ps[:P, h * 64:(h + 1) * 64],
                                                                        scalar=bsT[:P, h:h + 1], in1=u_[:P, h * 64:(h + 1) * 64],
                                                                        op0=ALU.add, op1=ALU.mult),
                             [psb, bsTb, ub_], [mxb])
                    sg_, sgb_ = zbb.next()
                    rmsnorm(mx[:P, :], mxb, P, 512, sg_[:P, :], sgb_, grep[:P, 5, 512:1024], grepb)
                    c.dma(POOL, sgu_s[r0:r0 + P, :], sg_[:P, :], reads=[sgb_])

                for b in range(NB):
                    for t in range(NT):
                        r0 = b * T + t * 128
                        stage1_tile(xp[r0:r0 + 128, :], 128, r0, True, b, t * 128)

                P = NSO
                xt_, xb_ = xin.next()
                c.dma(SP, xt_[:P, :], xso[:, :], writes=[xb_])
                xn_, xnb_ = xnb.next()
                rmsnorm(xt_[:P, :], xb_, P, D, xn_[:P, :], xnb_, grep[:P, 0, :], grepb)
                xT_, xTb_ = transpose_tile(xn_, xnb_, P, 8)
                ta, tab = ga.next(); tb_, tbb = gb_.next()
                ps, psb = mm_tok(xT_, xTb_, P, win, 8, 1544, 512)
                u_, ub_ = ubuf.next()
                gelu_from_psum(ps, psb, P, 512, u_[:P, :], ub_, ta, tab, tb_, tbb)
                ta, tab = ga.next(); tb_, tbb = gb_.next()
                ps, psb = mm_tok(xT_, xTb_, P, win, 8, 2056, 512)
                z_, zb_ = zbuf.next()
                gelu_from_psum(ps, psb, P, 512, z_[:P, :], zb_, ta, tab, tb_, tbb)
                zf, zfb = osb.next()
                rmsnorm(z_[:P, :], zb_, P, 512, zf[:P, :], zfb, gsv[:P, :], gsvb)
                c.dma(SP, zs_o[:, :], zf[:P, :], reads=[zfb])
                wsx, wsxb = const([NSO, 512], F32, ws00x.broadcast_to([NSO, 512]))
                bsx, bsxb = const([NSO, 512], F32, bs0x.broadcast_to([NSO, 512]))
                mx, mxb = osb.next()
                c.op(DVE, lambda e: e.tensor_tensor(out=mx[:P, :], in0=zf[:P, :], in1=wsx[:P, :], op=ALU.mult), [zfb, wsxb], [mxb])
                c.op(DVE, lambda e: e.tensor_tensor(out=mx[:P, :], in0=mx[:P, :], in1=bsx[:P, :], op=ALU.add), [mxb, bsxb], [mxb])
                c.op(DVE, lambda e: e.tensor_tensor(out=mx[:P, :], in0=mx[:P, :], in1=u_[:P, :], op=ALU.mult), [mxb, ub_], [mxb])
                rmsnorm(mx[:P, :], mxb, P, 512, mgs[:P, 512:1024], mgsb, grep[:P, 5, 512:1024], grepb)

                def barrier_all_dma(E):
                    for ds in c.dsem:
                        if ds[1] > 0:
                            E.wait(id(ds[0]), ds[0], ds[1], None)

                barrier_all_dma(SP)
                barrier_all_dma(POOL)

            with c.scope():
                kTh = Rot(c, 2, [64, T], BF16, p="kTh")
                qTh = Rot(c, 2, [64, T], BF16, p="qTh")
                vah = c.sb([128, NT, 8 * 65], BF16, "vah"); vahb = Buf()
                lfb = c.sb([128, NT * 8], F32, "lfb"); lfbb = Buf()
                Cc = c.sb([128, NT, 8], F32, "Cc"); Ccb = Buf()
                pre = c.sb([128, NT + 1, 8], F32, "pre"); preb = Buf()
                tbt = c.sb([128, NT, NG, 8], F32, "tbt"); tbtb = Buf()
                PTr = Rot(c, 4, [128, 256], BF16, p="PT")
                attsb = Rot(c, 2, [128, NT, 64], F32, p="att")
                for b in range(NB):
                    c.dma(SP, vah[:], va_s[b * T:(b + 1) * T, :].rearrange("(t s) c -> s t c", s=128), writes=[vahb])
                    c.dma(SP, lfb[:].rearrange("p (t h) -> p t h", h=8), lf_s[b * T:(b + 1) * T, :].rearrange("(t s) h -> s t h", s=128),
                          writes=[lfbb])
                    ps, psb = pA.next()
                    c.op(PE, lambda e: e.matmul(ps[:, :NT * 8], lhsT=mask_f[:], rhs=lfb[:], start=True, stop=True), [mask_fb, lfbb], [psb])
                    ps2, psb2 = pA.next()
                    c.op(PE, lambda e: e.matmul(ps2[:, :NT * 8], lhsT=ones_f[:], rhs=lfb[:], start=True, stop=True), [ones_fb, lfbb], [psb2])
                    c.op(DVE, lambda e: e.memset(pre[:, 0, :], 0.0), [], [preb])
                    for j in range(NT):
                        c.op(DVE, lambda e, j=j: e.tensor_tensor(out=pre[:, j + 1, :], in0=pre[:, j, :], in1=ps2[:, j * 8:(j + 1) * 8], op=ALU.add),
                             [preb, psb2], [preb])
                    c.op(DVE, lambda e: e.tensor_tensor(out=Cc[:], in0=ps[:, :NT * 8].rearrange("p (t h) -> p t h", h=8), in1=pre[:, 0:NT, :], op=ALU.add),
                         [psb, preb], [Ccb])
                    for g in range(NG):
                        for j in range(2 * g + 2):
                            c.op(DVE, lambda e, j=j, g=g: e.tensor_tensor(out=tbt[:, j, g, :], in0=pre[:, 2 * g + 1, :], in1=Cc[:, j, :], op=ALU.subtract),
                                 [preb, Ccb], [tbtb])
                    for h in range(8):
                        kt_, ktb_ = kTh.next()
                        qt_, qtb_ = qTh.next()
                        c.dma(SP, kt_[:], kT_s[b * 512 + h * 64:b * 512 + (h + 1) * 64, :], writes=[ktb_])
                        c.dma(SP, qt_[:], qT_s[b * 512 + h * 64:b * 512 + (h + 1) * 64, :], writes=[qtb_])
                        at_, atb_ = attsb.next()
                        accs = [pS.slots[0], pS.slots[1]]
                        items = [(g, j) for g in range(NG) for j in range(2 * g + 2)]
                        LA = 2
                        pend = []

                        def emit_qk(g, j):
                            lo = 1 if j == 2 * g + 1 else 0
                            ncol = (2 - lo) * 128
                            q0 = (2 * g + lo) * 128
                            ps, psb = pA.next()
                            c.op(PE, lambda e: e.matmul(ps[:, :ncol], lhsT=kt_[:, j * 128:(j + 1) * 128],
                                                        rhs=qt_[:, q0:q0 + ncol], start=True, stop=True),
                                 [ktb_, qtb_], [psb])
                            pt_, ptb_ = PTr.next()
                            c.op(ACT, lambda e: e.activation(out=pt_[:, :ncol], in_=ps[:, :ncol], func=AF.Exp,
                                                             bias=tbt[:, j, g, h:h + 1], scale=1.0),
                                 [psb, tbtb], [ptb_])
                            if j >= 2 * g:
                                c.op(DVE, lambda e: e.tensor_tensor(out=pt_[:, 0:128], in0=pt_[:, 0:128], in1=mask_b[:], op=ALU.mult),
                                     [ptb_, mask_bb], [ptb_])
                            return (g, j, lo, pt_, ptb_)

                        def emit_pv(g, j, lo, pt_, ptb_):
                            for ii in range(lo, 2):
                                i = 2 * g + ii
                                a_, ab_ = accs[ii]
                                cs = (ii - lo) * 128
                                c.op(PE, lambda e: e.matmul(a_[:, 0:65], lhsT=pt_[:, cs:cs + 128],
                                                            rhs=vah[:, j, h * 65:(h + 1) * 65],
                                                            start=(j == 0), stop=(j == i)),
                                     [ptb_, vahb], [ab_])
                            if j == 2 * g + 1:
                                for ii in range(2):
                                    i = 2 * g + ii
                                    a_, ab_ = accs[ii]
                                    r_, rb_ = rl.next()
                                    c.op(DVE, lambda e: e.reciprocal(out=r_[:], in_=a_[:, 64:65]), [ab_], [rb_])
                                    c.op(DVE, lambda e: e.tensor_scalar(out=at_[:, i, :], in0=a_[:, 0:64], scalar1=r_[:], scalar2=None, op0=ALU.mult),
                                         [ab_, rb_], [atb_])

                        for idx in range(len(items) + LA):
                            if idx < len(items):
                                pend.append(emit_qk(*items[idx]))
                            if idx >= LA:
                                emit_pv(*pend[idx - LA])
                        c.dma(POOL, att_s[b * T:(b + 1) * T, h * 64:(h + 1) * 64].rearrange("(t s) d -> s t d", s=128), at_[:], reads=[atb_])

                barrier_all_dma(SP)
                barrier_all_dma(POOL)

            with c.scope():
                alloc_w(24576)
                wo = load_w(0, w_o, D, D)
                wcq = load_w(8 * D, w_cq, D, D)
                wco = load_w(16 * D, w_co, D, D)
                attin = Rot(c, 2, [128, 512], F32, p="ai")
                mrg = Rot(c, 2, [128, D], BF16, p="mg")
                x1r = Rot(c, 3, [128, D], F32, p="x1")
                qcT = Rot(c, 3, [128, 8, 128], BF16, p="qcT")
                PcT = Rot(c, 2, [128, 2, 128], BF16, p="PcT")
                ocr = Rot(c, 2, [128, D], BF16, p="oc")
                x2r = Rot(c, 2, [128, D], F32, p="x2")
                def h1(b, t):
                    r0 = b * T + t * 128
                    ai, aib = attin.next()
                    c.dma(SP, ai[:], att_s[r0:r0 + 128, :], writes=[aib])
                    mg, mgb = mrg.next()
                    c.dma(SP, mg[:, 512:1024], sgu_s[r0:r0 + 128, :], writes=[mgb])
                    rmsnorm(ai[:], aib, 128, 512, mg[:, 0:512], mgb, grep[:, 5, 0:512], grepb)
                    mT, mTb = transpose_tile(mg, mgb, 128, 8)
                    xt_, xb_ = xin.next()
                    c.dma(SP, xt_[:], xp[r0:r0 + 128, :], writes=[xb_])
                    x1, x1b = x1r.next()
                    for cc in range(2):
                        ps, psb = mm_tok(mT, mTb, 128, wo, 8, cc * 512, 512)
                        c.op(DVE, lambda e, cc=cc: e.tensor_tensor(out=x1[:, cc * 512:(cc + 1) * 512], in0=ps[:, :], in1=xt_[:, cc * 512:(cc + 1) * 512], op=ALU.add),
                             [psb, xb_], [x1b])
                    xn_, xnb_ = xnb.next()
                    rmsnorm(x1[:], x1b, 128, D, xn_[:], xnb_, grep[:, 1, :], grepb)
                    xT_, xTb_ = transpose_tile(xn_, xnb_, 128, 8)
                    qc, qcb = qcT.next()
                    for blk in range(8):
                        ps, psb = pX.next()
                        for k in range(8):
                            c.op(PE, lambda e, k=k, blk=blk: e.matmul(ps[:, :128], lhsT=wcq[:, k, blk * 128:(blk + 1) * 128], rhs=xT_[:, k, :],
                                                                      start=(k == 0), stop=(k == 7)), [wregb, xTb_], [psb])
                        c.op(ACT, lambda e, blk=blk: e.activation(out=qc[:, blk, :], in_=ps[:, :128], func=AF.Copy, scale=1.0 / 16.0), [psb], [qcb])
                    return (b, r0, x1, x1b, qc, qcb)

                def h2(b, r0, x1, x1b, qc, qcb):
                    oc, ocb = ocr.next()
                    for m in range(4):
                        pc, pcb = PcT.next()
                        for mb in range(2):
                            ps, psb = pX.next()
                            for cc in range(2):
                                c.op(PE, lambda e, cc=cc, mb=mb, m=m: e.matmul(ps[:, :128], lhsT=mkT[:, b, 2 * m + cc, mb * 128:(mb + 1) * 128],
                                                                               rhs=qc[:, 2 * m + cc, :], start=(cc == 0), stop=(cc == 1)),
                                     [mkTb, qcb], [psb])
                            c.op(ACT, lambda e, mb=mb: e.activation(out=pc[:, mb, :], in_=ps[:, :128], func=AF.Exp), [psb], [pcb])
                        ps, psb = pX.next()
                        for mb in range(2):
                            c.op(PE, lambda e, mb=mb, m=m: e.matmul(ps[:, :257], lhsT=pc[:, mb, :], rhs=mva[:, b, mb, m * 257:(m + 1) * 257],
                                                                    start=(mb == 0), stop=(mb == 1)), [pcb, mvab], [psb])
                        r_, rb_ = rl.next()
                        c.op(DVE, lambda e: e.reciprocal(out=r_[:], in_=ps[:, 256:257]), [psb], [rb_])
                        c.op(DVE, lambda e, m=m: e.tensor_scalar(out=oc[:, m * 256:(m + 1) * 256], in0=ps[:, 0:256], scalar1=r_[:], scalar2=None, op0=ALU.mult),
                             [psb, rb_], [ocb])
                    oT, oTb = transpose_tile(oc, ocb, 128, 8)
                    x2, x2b = x2r.next()
                    for cc in range(2):
                        ps, psb = mm_tok(oT, oTb, 128, wco, 8, cc * 512, 512)
                        c.op(DVE, lambda e, cc=cc: e.tensor_tensor(out=x2[:, cc * 512:(cc + 1) * 512], in0=ps[:, :], in1=x1[:, cc * 512:(cc + 1) * 512], op=ALU.add),
                             [psb, x1b], [x2b])
                    c.dma(POOL, x2_s[r0:r0 + 128, :], x2[:], reads=[x2b])


                tiles34 = [(b, t) for b in range(NB) for t in range(NT)]
                st34 = []
                for n in range(len(tiles34) + 1):
                    if n < len(tiles34):
                        st34.append(h1(*tiles34[n]))
                    if n >= 1:
                        h2(*st34[n - 1])

                with c.scope():
                    P = NSO
                    ai, aib = attin.next()
                    c.dma(SP, ai[:P, :], atto[:, :], writes=[aib])
                    rmsnorm(ai[:P, :], aib, P, 512, mgs[:P, 0:512], mgsb, grep[:P, 5, 0:512], grepb)
                    mT, mTb = transpose_tile(mgs, mgsb, P, 8)
                    xt_, xb_ = xin.next()
                    c.dma(SP, xt_[:P, :], xso[:, :], writes=[xb_])
                    x1, x1b = x1r.next()
                    for cc in range(2):
                        ps, psb = mm_tok(mT, mTb, P, wo, 8, cc * 512, 512)
                        c.op(DVE, lambda e, cc=cc: e.tensor_tensor(out=x1[:P, cc * 512:(cc + 1) * 512], in0=ps[:P, :], in1=xt_[:P, cc * 512:(cc + 1) * 512], op=ALU.add),
                             [psb, xb_], [x1b])
                    xn_, xnb_ = xnb.next()
                    rmsnorm(x1[:P, :], x1b, P, D, xn_[:P, :], xnb_, grep[:P, 1, :], grepb)
                    xT_, xTb_ = transpose_tile(xn_, xnb_, P, 8)
                    qs = c.sb([NSO, D], F32, "qs"); qsb = Buf()
                    for cc in range(2):
                        ps, psb = mm_tok(xT_, xTb_, P, wcq, 8, cc * 512, 512)
                        c.op(ACT, lambda e, cc=cc: e.activation(out=qs[:P, cc * 512:(cc + 1) * 512], in_=ps[:P, :], func=AF.Copy, scale=1.0 / 16.0), [psb], [qsb])
                    sel_t, sel_b = const([NSO, NSO * 128], F32, sel4)
                    K2 = c.sb([128, 2, D], F32, "K2"); K2b = Buf()
                    V2 = c.sb([128, 2, D], F32, "V2"); V2b = Buf()
                    qrep = c.sb([128, D], F32, "qrep"); qrepb = Buf()
                    Sx = c.sb([128, 2, 4], F32, "Sx"); Sxb = Buf()
                    Pm = c.sb([128, 2, 4], F32, "Pm"); Pmb = Buf()
                    o4 = c.sb([4, D], F32, "o4"); o4b = Buf()
                    ocsb = Buf()
                    for b in range(NSO):
                        c.dma(SP, K2[:], cmk[b * NMEM:(b + 1) * NMEM, :].rearrange("(m p) d -> p m d", p=128), writes=[K2b])
                        c.dma(SP, V2[:], cmv[b * NMEM:(b + 1) * NMEM, :].rearrange("(m p) d -> p m d", p=128), writes=[V2b])
                        for cc in range(2):
                            ps, psb = pA.next()
                            c.op(PE, lambda e, cc=cc, b=b: e.matmul(ps[:, :512], lhsT=sel_t[:P, b * 128:(b + 1) * 128], rhs=qs[:P, cc * 512:(cc + 1) * 512],
                                                                    start=True, stop=True), [sel_b, qsb], [psb])
                            c.op(ACT, lambda e, cc=cc: e.copy(out=qrep[:, cc * 512:(cc + 1) * 512], in_=ps[:, :512]), [psb], [qrepb])
                        for mb in range(2):
                            c.op(DVE, lambda e, mb=mb: e.tensor_tensor(out=K2[:, mb, :], in0=K2[:, mb, :], in1=qrep[:], op=ALU.mult), [K2b, qrepb], [K2b])
                            c.op(DVE, lambda e, mb=mb: e.tensor_reduce(out=Sx[:, mb, :], in_=K2[:, mb, :].rearrange("p (m f) -> p m f", m=4), axis=AX.X, op=ALU.add),
                                 [K2b], [Sxb])
                        c.op(ACT, lambda e: e.activation(out=Pm[:].rearrange("p a b -> p (a b)"), in_=Sx[:].rearrange("p a b -> p (a b)"), func=AF.Exp), [Sxb], [Pmb])
                        pso = []
                        for cc in range(2):
                            ps, psb = pA.next()
                            for mb in range(2):
                                c.op(PE, lambda e, cc=cc, mb=mb: e.matmul(ps[:4, :512], lhsT=Pm[:, mb, :], rhs=V2[:, mb, cc * 512:(cc + 1) * 512],
                                                                          start=(mb == 0), stop=(mb == 1)), [Pmb, V2b], [psb])
                            pso.append((ps, psb))
                        psd, psdb = pS.next()
                        for mb in range(2):
                            c.op(PE, lambda e, mb=mb: e.matmul(psd[:4, 0:1], lhsT=Pm[:, mb, :], rhs=ones_f[:, 0:1], start=(mb == 0), stop=(mb == 1)),
                                 [Pmb, ones_fb], [psdb])
                        r_, rb_ = rl.next()
                        c.op(DVE, lambda e: e.reciprocal(out=r_[:4, :], in_=psd[:4, 0:1]), [psdb], [rb_])
                        for cc in range(2):
                            ps, psb = pso[cc]
                            c.op(DVE, lambda e, cc=cc, ps=ps: e.tensor_scalar(out=o4[:4, cc * 512:(cc + 1) * 512], in0=ps[:4, :512], scalar1=r_[:4, :], scalar2=None, op0=ALU.mult),
                                 [psb, rb_], [o4b])
                        for m in range(4):
                            c.dma(SP, oc_s[b:b + 1, m * 256:(m + 1) * 256], o4[m:m + 1, m * 256:(m + 1) * 256], reads=[o4b], writes=[ocsb])
                    ocf, ocfb = x2r.next()
                    c.dma(SP, ocf[:P, :], oc_s[:, :], reads=[ocsb], writes=[ocfb])
                    oc, ocb = ocr.next()
                    c.op(DVE, lambda e: e.tensor_copy(out=oc[:P, :], in_=ocf[:P, :]), [ocfb], [ocb])
                    oT, oTb = transpose_tile(oc, ocb, P, 8)
                    x2, x2b = x2r.next()
                    for cc in range(2):
                        ps, psb = mm_tok(oT, oTb, P, wco, 8, cc * 512, 512)
                        c.op(DVE, lambda e, cc=cc: e.tensor_tensor(out=x2[:P, cc * 512:(cc + 1) * 512], in0=ps[:P, :], in1=x1[:P, cc * 512:(cc + 1) * 512], op=ALU.add),
                             [psb, x1b], [x2b])
                    c.dma(POOL, x2s_s[:, :], x2[:P, :], reads=[x2b])

                barrier_all_dma(SP)
                barrier_all_dma(POOL)

        with c.scope():
            alloc_w(8 * F2 + 22 * D)
            pmm[0] = pA
            grep5, grep5b = const([128, 2, D], F32, gvec[3:5, :].rearrange("(o g) d -> o g d", o=1).broadcast_to([128, 2, D]))
            wup = load_w(0, w_up, D, F2)
            wdn = load_w(8 * F2, w_down, DFF, D)
            with c.scope():
                CH = 256 if T % 256 == 0 else 128
                TPC = CH // 128
                halos = [(c.sb([128, 44, 2], F32, "halo"), Buf()) for _ in range(2)]
                corr = c.sb([128, 44, 2], F32, "corr"); corrb = Buf()
                ctm = c.sb([128, 44, 2], F32, "ctm"); ctmb = Buf()
                xT2r = Rot(c, 2, [128, 8, CH], BF16, p="xT2")
                accr = Rot(c, 5, [128, 2, CH], F32, p="acc")
                sgr = Rot(c, 3, [128, CH], F32, p="sg")
                hT = c.sb([128, 22, CH], BF16, "hT"); hTb = Buf()
                cvsb = Rot(c, 1, [2, 512], F32, p="cv")
                for b in range(NB):
                    c.op(DVE, lambda e: e.memset(halos[0][0][:], 0.0), [], [halos[0][1]])
                    for ch in range(T // CH):
                        xT2, xT2b = xT2r.next()
                        halo, halob = halos[ch % 2]
                        halo_n, halo_nb = halos[(ch + 1) % 2]
                        c.op(POOL, lambda e: e.tensor_tensor(out=ctm[:, :, 0], in0=halo[:, :, 1], in1=cw[:, :, 1], op=ALU.mult), [halob, cwb], [ctmb])
                        c.op(POOL, lambda e: e.tensor_tensor(out=ctm[:, :, 1], in0=halo[:, :, 0], in1=cw[:, :, 0], op=ALU.mult), [halob, cwb, ctmb], [ctmb])
                        c.op(POOL, lambda e: e.tensor_tensor(out=corr[:, :, 0], in0=ctm[:, :, 0], in1=ctm[:, :, 1], op=ALU.add), [ctmb], [corrb])
                        c.op(POOL, lambda e: e.tensor_tensor(out=corr[:, :, 1], in0=halo[:, :, 1], in1=cw[:, :, 0], op=ALU.mult), [halob, cwb, corrb], [corrb])
                        xts = []
                        for ti in range(TPC):
                            r0 = b * T + ch * CH + ti * 128
                            x2, x2b = xin.next()
                            c.dma(SP, x2[:], x2_s[r0:r0 + 128, :], writes=[x2b])
                            xn_, xnb_ = xnb.next()
                            rmsnorm(x2[:], x2b, 128, D, xn_[:], xnb_, grep5[:, 0, :], grep5b)
                            pt, ptb = pT.next()
                            for k in range(8):
                                c.op(PE, lambda e, k=k: e.transpose(out=pt[:, k * 128:(k + 1) * 128], in_=xn_[:, k * 128:(k + 1) * 128], identity=ident[:]),
                                     [xnb_, identb], [ptb])
                            c.op(DVE, lambda e, ti=ti: e.tensor_copy(out=xT2[:, :, ti * 128:(ti + 1) * 128],
                                                                     in_=pt[:, :].rearrange("p (c t) -> p c t", c=8)), [ptb], [xT2b])
                            xts.append((x2, x2b, r0))
                        def front(i):
                            ps, psb = pA.next()
                            for hh, fb in enumerate((i, 22 + i)):
                                for k in range(8):
                                    c.op(PE, lambda e, k=k, fb=fb, hh=hh: e.matmul(ps[:, hh * CH:(hh + 1) * CH], lhsT=wup[:, k, fb * 128:(fb + 1) * 128],
                                                                                   rhs=xT2[:, k, :], start=(k == 0), stop=(k == 7)),
                                         [wregb, xT2b], [psb])
                            ac, acb = accr.next()
                            for hh, fb in enumerate((i, 22 + i)):
                                c.op(ACT, lambda e, hh=hh, fb=fb: e.activation(out=ac[:, hh, :], in_=ps[:, hh * CH:(hh + 1) * CH], func=AF.Identity,
                                                                               bias=cw[:, fb, 3:4], scale=cw[:, fb, 2:3]), [psb, cwb], [acb])
                            for hh, fb in enumerate((i, 22 + i)):
                                c.op(ACT, lambda e, hh=hh, fb=fb: e.copy(out=halo_n[:, fb, :], in_=ps[:, (hh + 1) * CH - 2:(hh + 1) * CH]), [psb], [halo_nb])
                            for hh, fb in enumerate((i, 22 + i)):
                                c.op(DVE, lambda e, hh=hh, fb=fb: e.scalar_tensor_tensor(out=ac[:, hh, 1:CH], in0=ps[:, hh * CH:(hh + 1) * CH - 1], scalar=cw[:, fb, 1:2],
                                                                                         in1=ac[:, hh, 1:CH], op0=ALU.mult, op1=ALU.add), [psb, cwb, acb], [acb])
                                c.op(DVE, lambda e, hh=hh, fb=fb: e.scalar_tensor_tensor(out=ac[:, hh, 2:CH], in0=ps[:, hh * CH:(hh + 1) * CH - 2], scalar=cw[:, fb, 0:1],
                                                                                         in1=ac[:, hh, 2:CH], op0=ALU.mult, op1=ALU.add), [psb, cwb, acb], [acb])
                            return (i, ac, acb)

                        def mid(i, ac, acb):
                            for hh, fb in enumerate((i, 22 + i)):
                                c.op(POOL, lambda e, hh=hh, fb=fb: e.tensor_tensor(out=ac[:, hh, 0:2], in0=ac[:, hh, 0:2], in1=corr[:, fb, :], op=ALU.add),
                                     [corrb, acb], [acb])
                            sg, sgb = sgr.next()
                            c.op(ACT, lambda e: e.activation(out=sg[:], in_=ac[:, 0, :], func=AF.Silu), [acb], [sgb])
                            return (i, ac, acb, sg, sgb)

                        def tail(i, ac, acb, sg, sgb):
                            c.op(POOL, lambda e: e.tensor_tensor(out=hT[:, i, :], in0=sg[:], in1=ac[:, 1, :], op=ALU.mult), [sgb, acb], [hTb])

                        q1 = []
                        q2 = []
                        for i in range(22 + 2):
                            if i < 22:
                                q1.append(front(i))
                            if 1 <= i <= 22:
                                q2.append(mid(*q1[i - 1]))
                            if i >= 2:
                                tail(*q2[i - 2])
                        for ti, (x2, x2b, r0) in enumerate(xts):
                            for cc in range(2):
                                ps, psb = pA.next()
                                for fb in range(22):
                                    c.op(PE, lambda e, fb=fb, cc=cc, ti=ti: e.matmul(ps[:, :], lhsT=hT[:, fb, ti * 128:(ti + 1) * 128], rhs=wdn[:, fb, cc * 512:(cc + 1) * 512],
                                                                                   start=(fb == 0), stop=(fb == 21)), [hTb, wregb], [psb])
                                c.op(DVE, lambda e, cc=cc, x2=x2: e.tensor_tensor(out=x2[:, cc * 512:(cc + 1) * 512], in0=ps[:, :], in1=x2[:, cc * 512:(cc + 1) * 512], op=ALU.add),
                                     [psb, x2b], [x2b])
                            rmsnorm(x2[:], x2b, 128, D, x2[:], x2b, grep5[:, 1, :], grep5b)
                            c.dma(POOL, y_o[r0:r0 + 128, :], x2[:], reads=[x2b])
                        if ch == T // CH - 1:
                            for cc in range(11):
                                ps, psb = pS.next()
                                for k in range(8):
                                    c.op(PE, lambda e, k=k, cc=cc: e.matmul(ps[:2, :], lhsT=xT2[:, k, CH - 2:CH], rhs=wup[:, k, cc * 512:(cc + 1) * 512],
                                                                            start=(k == 0), stop=(k == 7)), [xT2b, wregb], [psb])
                                cv, cvb = cvsb.next()
                                c.op(ACT, lambda e: e.copy(out=cv[:, :], in_=ps[:2, :]), [psb], [cvb])
                                c.dma(POOL, cv_o[b * 2:b * 2 + 2, cc * 512:(cc + 1) * 512], cv[:, :], reads=[cvb])

            with c.scope():
                P = NSO
                x2, x2b = xin.next()
                c.dma(SP, x2[:P, :], x2s_s[:, :], writes=[x2b])
                xn_, xnb_ = xnb.next()
                rmsnorm(x2[:P, :], x2b, P, D, xn_[:P, :], xnb_, grep5[:P, 0, :], grep5b)
                xT_, xTb_ = transpose_tile(xn_, xnb_, P, 8)
                hid = c.sb([NSO, DFF], BF16, "hid"); hidb = Buf()
                hpr = Rot(c, 2, [NSO, 512], F32, p="hp")
                pvr = Rot(c, 1, [NSO, 2, 512], F32, p="pv")
                cwpr = Rot(c, 1, [NSO, 4, 512], F32, p="cwp")
                accr = Rot(c, 1, [NSO, 512], F32, p="acc")
                tmpr = Rot(c, 1, [NSO, 512], F32, p="tmp")
                sgr = Rot(c, 1, [NSO, 256], F32, p="sg")
                stc3 = stc.rearrange("(b r) f -> b r f", r=2)
                cvs3 = cvs_o.rearrange("(b r) f -> b r f", r=2)
                for i in range(11):
                    g0 = i * 256
                    v0 = DFF + i * 256
                    ps, psb = pA.next()
                    for (d0, c0) in ((0, g0), (256, v0)):
                        for k in range(8):
                            c.op(PE, lambda e, k=k, d0=d0, c0=c0: e.matmul(ps[:P, d0:d0 + 256], lhsT=xT_[:, k, :P], rhs=wup[:, k, c0:c0 + 256],
                                                                           start=(k == 0), stop=(k == 7)), [xTb_, wregb], [psb])
                    h_, hb_ = hpr.next()
                    c.op(ACT, lambda e: e.copy(out=h_[:P, :], in_=ps[:P, :]), [psb], [hb_])
                    pv_, pvb_ = pvr.next()
                    c.dma(SP, pv_[:P, :, 0:256], stc3[:, :, g0:g0 + 256], writes=[pvb_])
                    c.dma(SP, pv_[:P, :, 256:512], stc3[:, :, v0:v0 + 256], writes=[pvb_])
                    cw_, cwb_ = cwpr.next()
                    c.dma(SP, cw_[:P, :, 0:256], cwrow[:, g0:g0 + 256].rearrange("(o r) f -> o r f", o=1).broadcast_to([P, 4, 256]), writes=[cwb_])
                    c.dma(SP, cw_[:P, :, 256:512], cwrow[:, v0:v0 + 256].rearrange("(o r) f -> o r f", o=1).broadcast_to([P, 4, 256]), writes=[cwb_])
                    c.dma(POOL, cvs3[:, 0, g0:g0 + 256], pv_[:P, 1, 0:256], reads=[pvb_])
                    c.dma(POOL, cvs3[:, 0, v0:v0 + 256], pv_[:P, 1, 256:512], reads=[pvb_])
                    c.dma(POOL, cvs3[:, 1, g0:g0 + 256], h_[:P, 0:256], reads=[hb_])
                    c.dma(POOL, cvs3[:, 1, v0:v0 + 256], h_[:P, 256:512], reads=[hb_])
                    a_, ab_ = accr.next()
                    t_, tb2 = tmpr.next()
                    c.op(DVE, lambda e: e.tensor_tensor(out=a_[:P, :], in0=h_[:P, :], in1=cw_[:P, 2, :], op=ALU.mult), [hb_, cwb_], [ab_])
                    c.op(DVE, lambda e: e.tensor_tensor(out=a_[:P, :], in0=a_[:P, :], in1=cw_[:P, 3, :], op=ALU.add), [ab_, cwb_], [ab_])
                    c.op(DVE, lambda e: e.tensor_tensor(out=t_[:P, :], in0=pv_[:P, 1, :], in1=cw_[:P, 1, :], op=ALU.mult), [pvb_, cwb_], [tb2])
                    c.op(DVE, lambda e: e.tensor_tensor(out=a_[:P, :], in0=a_[:P, :], in1=t_[:P, :], op=ALU.add), [ab_, tb2], [ab_])
                    c.op(DVE, lambda e: e.tensor_tensor(out=t_[:P, :], in0=pv_[:P, 0, :], in1=cw_[:P, 0, :], op=ALU.mult), [pvb_, cwb_], [tb2])
                    c.op(DVE, lambda e: e.tensor_tensor(out=a_[:P, :], in0=a_[:P, :], in1=t_[:P, :], op=ALU.add), [ab_, tb2], [ab_])
                    s_, sb_ = sgr.next()
                    c.op(ACT, lambda e: e.activation(out=s_[:P, :], in_=a_[:P, 0:256], func=AF.Silu), [ab_], [sb_])
                    c.op(DVE, lambda e, i=i: e.tensor_tensor(out=hid[:P, i * 256:(i + 1) * 256], in0=s_[:P, :], in1=a_[:P, 256:512], op=ALU.mult),
                         [sb_, ab_], [hidb])
                hTs = c.sb([128, 22, NSO], BF16, "hTs"); hTsb = Buf()
                for (c0, n) in ((0, 8), (8, 8), (16, 6)):
                    xt3, xt3b = transpose_tile(hid[:, c0 * 128:(c0 + n) * 128], hidb, P, n)
                    c.op(DVE, lambda e, c0=c0, n=n, xt3=xt3: e.tensor_copy(out=hTs[:, c0:c0 + n, :], in_=xt3[:, :n, :P]), [xt3b], [hTsb])
                y_, yb_ = xin.next()
                for cc in range(2):
                    ps, psb = pA.next()
                    for fb in range(22):
                        c.op(PE, lambda e, fb=fb, cc=cc: e.matmul(ps[:P, :], lhsT=hTs[:, fb, :], rhs=wdn[:, fb, cc * 512:(cc + 1) * 512],
                                                                  start=(fb == 0), stop=(fb == 21)), [hTsb, wregb], [psb])
                    c.op(DVE, lambda e, cc=cc: e.tensor_tensor(out=y_[:P, cc * 512:(cc + 1) * 512], in0=ps[:P, :], in1=x2[:P, cc * 512:(cc + 1) * 512], op=ALU.add),
                         [psb, x2b], [yb_])
                rmsnorm(y_[:P, :], yb_, P, D, y_[:P, :], yb_, grep5[:P, 1, :], grep5b)
                c.dma(POOL, ys_o[:, :], y_[:P, :], reads=[yb_])

        c.finish()
    return nc


def build_samp(NS, NPG, NPHYS):
    nc = bass.Bass("TRN2", target_bir_lowering=False)

    def din(name, shape, dt=F32):
        return nc.dram_tensor(name, list(shape), dt, kind="ExternalInput").ap()

    def dout(name, shape, dt=F32):
        return nc.dram_tensor(name, list(shape), dt, kind="ExternalOutput").ap()

    xs = din("xs", [NS, D])
    w_in_c = din("w_in_c", [D, 193])
    b_f_c = din("b_f_c", [1, 1])
    g_mix = din("g_mix", [1, D])
    kc = din("kc", [NPHYS, 128 * 64])
    vc = din("vc", [NPHYS, 128 * 64])
    lfc = din("lfc", [NPHYS, 128])
    ptT = din("ptT", [NPG, NS], I32)
    ident_d = din("ident", [128, 128])
    sel_d = din("sel", [NS, NS * 128])
    mgt_d = din("mask_gt", [128, 128])
    att_o = dout("att_s", [NS, 64])
    ks_o = dout("ks", [NS, 64])
    vs_o = dout("vs", [NS, 64])
    lfs_o = dout("lfs", [NS, 1])
    res_s = nc.dram_tensor("res_s", [NS, 65], F32, kind="Internal").ap()
    PG = NPG

    with contextlib.ExitStack() as es:
        c = Ctx(nc, es)
        c.setup()
        PE, ACT, DVE, POOL, SP = c.PE, c.ACT, c.DVE, c.POOL, c.SP
        es.enter_context(nc.Block())

        def const(shape, dt, src_ap, q=None):
            t = c.sb(shape, dt, "c")
            b = Buf()
            c.dma(q or SP, t[:], src_ap, writes=[b])
            return t, b

        ident_f, ident_fb = const([128, 128], F32, ident_d)
        ident = c.sb([128, 128], BF16, "c"); identb = Buf()
        c.op(DVE, lambda e: e.tensor_copy(out=ident[:], in_=ident_f[:]), [ident_fb], [identb])
        sel_t, sel_b = const([NS, NS * 128], F32, sel_d)
        mgt, mgtb = const([128, 128], F32, mgt_d)
        ones_f = c.sb([128, 1], F32, "c"); ones_fb = Buf()
        c.op(DVE, lambda e: e.memset(ones_f[:], 1.0), [], [ones_fb])
        gm, gmb = const([NS, D], F32, g_mix.broadcast_to([NS, D]))
        bfc, bfcb = const([NS, 1], F32, b_f_c.broadcast_to([NS, 1]))
        pt_sb, pt_sbb = const([PG, NS], I32, ptT)
        wc = c.sb([128, 8, 193], BF16, "wc"); wcb = Buf()
        for k in range(8):
            c.dma(POOL, wc[:, k, :], w_in_c[k * 128:(k + 1) * 128, :], writes=[wcb])
        pA = Rot(c, 3, [128, 512], F32, psum=True, p="pa")
        pT = Rot(c, 1, [128, 1024], BF16, psum=True, p="pt")

        x_t = c.sb([NS, D], F32, "x"); xb = Buf()
        c.dma(SP, x_t[:], xs, writes=[xb])
        jt = c.sb([NS, D], BF16, "j"); jb = Buf()
        ss = c.sb([NS, 1], F32, "ss"); ssb = Buf()
        c.op(DVE, lambda e: e.memset(ss[:], 0.0), [], [ssb])
        c.op(ACT, lambda e: e.activation(out=jt[:], in_=x_t[:], func=AF.Square, accum_out=ss[:]), [xb, ssb], [jb, ssb])
        rs = c.sb([NS, 1], F32, "rs"); rsb = Buf()
        c.op(ACT, lambda e: e.activation(out=rs[:], in_=ss[:], func=AF.Sqrt, scale=1.0 / D, bias=EPS), [ssb], [rsb])
        rd = c.sb([NS, 1], F32, "rd"); rdb = Buf()
        c.op(DVE, lambda e: e.reciprocal(out=rd[:], in_=rs[:]), [rsb], [rdb])
        xn = c.sb([NS, D], BF16, "xn"); xnb_ = Buf()
        c.op(DVE, lambda e: e.scalar_tensor_tensor(out=xn[:], in0=x_t[:], scalar=rd[:], in1=gm[:], op0=ALU.mult, op1=ALU.mult),
             [xb, rdb, gmb], [xnb_])
        pt, ptb = pT.next()
        for ch in range(8):
            c.op(PE, lambda e, ch=ch: e.transpose(out=pt[:, ch * 128:ch * 128 + NS], in_=xn[:NS, ch * 128:(ch + 1) * 128], identity=ident[:NS, :NS]),
                 [xnb_, identb], [ptb])
        xT = c.sb([128, 8, NS], BF16, "xT"); xTb = Buf()
        c.op(DVE, lambda e: e.tensor_copy(out=xT[:], in_=pt[:, :].rearrange("p (c t) -> p c t", c=8)[:, :, :NS]), [ptb], [xTb])
        ps, psb = pA.next()
        for k in range(8):
            c.op(PE, lambda e, k=k: e.matmul(ps[:NS, :193], lhsT=xT[:, k, :], rhs=wc[:, k, :], start=(k == 0), stop=(k == 7)), [xTb, wcb], [psb])
        qx = c.sb([NS, 65], F32, "qx"); qxb = Buf()
        ksb = c.sb([NS, 64], F32, "ks"); ksbb = Buf()
        vsb = c.sb([NS, 64], F32, "vs"); vsbb = Buf()
        c.op(ACT, lambda e: e.activation(out=qx[:, 0:64], in_=ps[:NS, 0:64], func=AF.Copy, scale=0.125), [psb], [qxb])
        c.op(ACT, lambda e: e.copy(out=ksb[:], in_=ps[:NS, 64:128]), [psb], [ksbb])
        c.op(ACT, lambda e: e.copy(out=vsb[:], in_=ps[:NS, 128:192]), [psb], [vsbb])
        f1 = c.sb([NS, 1], F32, "f1"); f1b = Buf()
        c.op(DVE, lambda e: e.tensor_tensor(out=f1[:], in0=ps[:NS, 192:193], in1=bfc[:], op=ALU.add), [psb, bfcb], [f1b])
        c.op(ACT, lambda e: e.activation(out=f1[:], in_=f1[:], func=AF.Exp, scale=-1.0), [f1b], [f1b])
        c.op(ACT, lambda e: e.activation(out=f1[:], in_=f1[:], func=AF.Ln, bias=1.0), [f1b], [f1b])
        c.op(DVE, lambda e: e.tensor_scalar(out=qx[:, 64:65], in0=f1[:], scalar1=-1.0, scalar2=None, op0=ALU.mult), [f1b, qxb], [qxb])
        c.dma(SP, ks_o, ksb[:], reads=[ksbb])
        c.dma(SP, vs_o, vsb[:], reads=[vsbb])
        c.dma(SP, lfs_o, qx[:, 64:65], reads=[qxb])

        Kr = Rot(c, 2, [128, 8192], F32, p="K")
        Vr = Rot(c, 2, [128, 8192], F32, p="V")
        LFr = Rot(c, 2, [128, 128], F32, p="LF")
        cA = c.sb([128, 128], F32, "cA"); cAb = Buf()
        cB = c.sb([128, 128], F32, "cB"); cBb = Buf()
        qrep = Rot(c, 2, [128, 65], F32, p="qrep")
        sc = Rot(c, 2, [128, 128], F32, p="sc")
        pp = Rot(c, 2, [128, 128], F32, p="pp")
        base = Rot(c, 2, [128, 1], F32, p="base")
        red = Rot(c, 2, [128, 65], F32, p="red")
        resrow = c.sb([1, NS * 65], F32, "resrow"); resrowb = Buf()
        import os
        DBG = int(os.environ.get('SAMP_DBG', '0'))
        for b in range(NS if DBG != 1 else 0):
            K_, Kb_ = Kr.next()
            V_, Vb_ = Vr.next()
            L_, Lb_ = LFr.next()
            idx = pt_sb[:PG, b:b + 1]
            c.dma(POOL, K_[:PG, :], kc[:, :], reads=[pt_sbb], writes=[Kb_], indirect=idx)
            c.dma(POOL, V_[:PG, :], vc[:, :], reads=[pt_sbb], writes=[Vb_], indirect=idx)
            c.dma(POOL, L_[:PG, :], lfc[:, :], reads=[pt_sbb], writes=[Lb_], indirect=idx)
            if DBG == 2:
                continue
            ps, psb = pA.next()
            c.op(PE, lambda e, b=b: e.matmul(ps[:PG, :65], lhsT=sel_t[:, b * 128:b * 128 + PG], rhs=qx[:, :], start=True, stop=True), [sel_b, qxb], [psb])
            qr, qrb = qrep.next()
            c.op(ACT, lambda e: e.copy(out=qr[:PG, :], in_=ps[:PG, :65]), [psb], [qrb])
            K3 = K_[:PG, :].rearrange("p (s d) -> p s d", d=64)
            c.op(DVE, lambda e: e.tensor_tensor(out=K3, in0=K3, in1=qr[:PG, 0:64].unsqueeze(1).broadcast_to([PG, 128, 64]), op=ALU.mult),
                 [Kb_, qrb], [Kb_])
            if DBG == 3:
                continue
            s_, sb_ = sc.next()
            c.op(DVE, lambda e: e.tensor_reduce(out=s_[:PG, :], in_=K3, axis=AX.X, op=ALU.add), [Kb_], [sb_])
            if DBG == 4:
                continue
            src, srcb = L_, Lb_
            dsts = [(cA, cAb), (cB, cBb)]
            di = 0
            k = 1
            while k < 128:
                dst, dstb = dsts[di]
                di ^= 1
                c.op(DVE, lambda e, k=k, dst=dst, src=src: e.tensor_copy(out=dst[:PG, 0:k], in_=src[:PG, 0:k]), [srcb], [dstb])
                c.op(DVE, lambda e, k=k, dst=dst, src=src: e.tensor_tensor(out=dst[:PG, k:128], in0=src[:PG, k:128], in1=src[:PG, 0:128 - k], op=ALU.add),
                     [srcb, dstb], [dstb])
                src, srcb = dst, dstb
                k *= 2
            cin, cinb = src, srcb
            ps2, psb2 = pA.next()
            c.op(PE, lambda e: e.matmul(ps2[:PG, 0:1], lhsT=mgt[:PG, :PG], rhs=cin[:PG, 127:128], start=True, stop=True), [mgtb, cinb], [psb2])
            bs_, bsb_ = base.next()
            c.op(DVE, lambda e: e.tensor_tensor(out=bs_[:PG, :], in0=ps2[:PG, 0:1], in1=cin[:PG, 127:128], op=ALU.add), [psb2, cinb], [bsb_])
            c.op(DVE, lambda e: e.tensor_tensor(out=bs_[:PG, :], in0=bs_[:PG, :], in1=qr[:PG, 64:65], op=ALU.add), [bsb_, qrb], [bsb_])
            c.op(DVE, lambda e: e.tensor_tensor(out=s_[:PG, :], in0=s_[:PG, :], in1=cin[:PG, :], op=ALU.subtract), [sb_, cinb], [sb_])
            r_, rb_ = red.next()
            c.op(DVE, lambda e: e.memset(r_[:PG, 64:65], 0.0), [], [rb_])
            p_, pb_ = pp.next()
            c.op(ACT, lambda e: e.activation(out=p_[:PG, :], in_=s_[:PG, :], func=AF.Exp, bias=bs_[:PG, 0:1], scale=1.0, accum_out=r_[:PG, 64:65]),
                 [sb_, bsb_, rb_], [pb_, rb_])
            if DBG == 5:
                continue
            V3 = V_[:PG, :].rearrange("p (s d) -> p s d", d=64)
            c.op(DVE, lambda e: e.tensor_tensor(out=V3, in0=V3, in1=p_[:PG, :].unsqueeze(2).broadcast_to([PG, 128, 64]), op=ALU.mult),
                 [Vb_, pb_], [Vb_])
            c.op(DVE, lambda e: e.tensor_reduce(out=r_[:PG, 0:64], in_=V_[:PG, :].rearrange("p (s d) -> p d s", d=64), axis=AX.X, op=ALU.add),
                 [Vb_, rb_], [rb_])
            ps3, psb3 = pA.next()
            c.op(PE, lambda e: e.matmul(ps3[:1, :65], lhsT=ones_f[:PG, 0:1], rhs=r_[:PG, :], start=True, stop=True), [ones_fb, rb_], [psb3])
            c.op(ACT, lambda e, b=b: e.copy(out=resrow[0:1, b * 65:(b + 1) * 65], in_=ps3[:1, :65]), [psb3], [resrowb])
        ressb = Buf()
        c.dma(SP, res_s.rearrange("(o b) c -> o (b c)", o=1), resrow[:], reads=[resrowb], writes=[ressb])
        res = c.sb([NS, 65], F32, "res"); resb = Buf()
        c.dma(SP, res[:], res_s, reads=[ressb], writes=[resb])
        pr = c.sb([NS, 64], F32, "pr"); prb = Buf()
        c.op(DVE, lambda e: e.tensor_tensor(out=pr[:], in0=qx[:, 0:64], in1=ksb[:], op=ALU.mult), [qxb, ksbb], [prb])
        sn = c.sb([NS, 1], F32, "sn"); snb = Buf()
        c.op(DVE, lambda e: e.tensor_reduce(out=sn[:], in_=pr[:], axis=AX.X, op=ALU.add), [prb], [snb])
        c.op(ACT, lambda e: e.activation(out=sn[:], in_=sn[:], func=AF.Exp), [snb], [snb])
        num = c.sb([NS, 64], F32, "num"); numb = Buf()
        c.op(DVE, lambda e: e.scalar_tensor_tensor(out=num[:], in0=vsb[:], scalar=sn[:], in1=res[:, 0:64], op0=ALU.mult, op1=ALU.add),
             [vsbb, snb, resb], [numb])
        den = c.sb([NS, 1], F32, "den"); denb = Buf()
        c.op(DVE, lambda e: e.tensor_tensor(out=den[:], in0=res[:, 64:65], in1=sn[:], op=ALU.add), [resb, snb], [denb])
        c.op(DVE, lambda e: e.reciprocal(out=den[:], in_=den[:]), [denb], [denb])
        c.op(DVE, lambda e: e.tensor_scalar(out=num[:], in0=num[:], scalar1=den[:], scalar2=None, op0=ALU.mult), [numb, denb], [numb])
        c.dma(SP, att_o, num[:], reads=[numb])
        c.finish()
    return nc


_CACHE = {}


def _consts():
    ident = np.eye(128, dtype=np.float32)
    ii = np.arange(128)
    mask_le = (ii[:, None] <= ii[None, :]).astype(np.float32)
    mask_gt = (ii[:, None] > ii[None, :]).astype(np.float32)
    return ident, mask_le, mask_gt


def _f(a):
    return np.ascontiguousarray(a, dtype=np.float32)


def _sel(n):
    sel = np.zeros((n, n, 128), np.float32)
    for b in range(n):
        sel[b, b, :] = 1.0
    return sel.reshape(n, n * 128)


def run_samp(inp, ncores=8):
    NS = inp["x_sample"].shape[0]
    NPG = inp["page_table"].shape[1]
    NPHYS = inp["cache_k"].shape[1]
    key = ("samp", NS, NPG, NPHYS)
    if key not in _CACHE:
        _CACHE[key] = build_samp(NS, NPG, NPHYS)
    nc = _CACHE[key]
    ident, mask_le, mask_gt = _consts()
    w_in = inp["w_in"][0]
    sel = _sel(NS)
    ptT = np.ascontiguousarray(inp["page_table"].T.astype(np.int32))
    xs = _f(inp["x_sample"][:, 0, :])
    in_maps = []
    for h in range(ncores):
        cols = np.concatenate([np.arange(h * 64, (h + 1) * 64), 512 + np.arange(h * 64, (h + 1) * 64),
                               1024 + np.arange(h * 64, (h + 1) * 64), np.array([1536 + h])])
        in_maps.append({
            "xs": xs, "w_in_c": _f(w_in[:, cols]), "b_f_c": _f(inp["b_f"][:, h:h + 1]), "g_mix": _f(inp["g_mix"]),
            "kc": _f(inp["cache_k"][0][:, :, h, :].reshape(NPHYS, 128 * 64)),
            "vc": _f(inp["cache_v"][0][:, :, h, :].reshape(NPHYS, 128 * 64)),
            "lfc": _f(inp["cache_logf"][0][:, :, h]),
            "ptT": ptT, "ident": ident, "sel": sel, "mask_gt": mask_gt,
        })
    return run_bass_kernel_spmd(nc, in_maps, core_ids=list(range(ncores))).results


def run_main(inp, att_all, ncores=8):
    B, T, _ = inp["x_prompt"].shape
    NB = B // ncores
    NS = inp["x_sample"].shape[0]
    NSO = NS // ncores
    key = ("main", NB, T, NSO)
    if key not in _CACHE:
        _CACHE[key] = build_main(NB, T, NSO)
    nc = _CACHE[key]
    ident, mask_le, mask_gt = _consts()
    gvec = np.stack([inp["g_mix"][0], inp["g_cross"][0], inp["g_mem"][0], inp["g_ffn"][0], inp["g_final"],
                     np.concatenate([inp["g_att_out"][0], inp["g_sgu_out"][0]])]).astype(np.float32)
    cw4 = np.concatenate([inp["conv_w"][0], inp["conv_b"]], axis=0)
    cwT = _f(cw4.reshape(4, 44, 128).transpose(2, 1, 0))
    ws00x = _f(np.repeat(inp["w_s"][0][:, 0, 0], 64)[None, :])
    bs0x = _f(np.repeat(inp["b_s"][0][:, 0], 64)[None, :])
    sel4 = _sel(NSO)
    shared = {
        "w_in": _f(inp["w_in"][0]), "w_o": _f(inp["w_o"][0]), "w_cq": _f(inp["w_cq"][0]), "w_ck": _f(inp["w_ck"][0]),
        "w_cv": _f(inp["w_cv"][0]), "w_co": _f(inp["w_co"][0]), "w_up": _f(inp["w_up"][0]), "w_down": _f(inp["w_down"][0]),
        "gvec": gvec, "g_sgu_v": _f(inp["g_sgu_v"]), "b_f": _f(inp["b_f"]),
        "w_sT": _f(inp["w_s"][0].transpose(0, 2, 1)), "b_sT": _f(inp["b_s"][0].T), "cwT": cwT,
        "ident": ident, "mask_le": mask_le, "sel4": sel4, "ws00x": ws00x, "bs0x": bs0x, "cwrow": _f(cw4),
    }
    in_maps = []
    for ci in range(ncores):
        m = dict(shared)
        sb = slice(ci * NSO, (ci + 1) * NSO)
        m.update({
            "xp": _f(inp["x_prompt"][ci * NB:(ci + 1) * NB].reshape(NB * T, D)),
            "memp": _f(inp["mem_prompt"][ci * NB:(ci + 1) * NB].reshape(NB * NMEM, D)),
            "xso": _f(inp["x_sample"][sb, 0, :]),
            "atto": _f(att_all[sb]),
            "cmk": _f(inp["cache_mem_k"][0][sb].reshape(NSO * NMEM, D)),
            "cmv": _f(inp["cache_mem_v"][0][sb].reshape(NSO * NMEM, D)),
            "stc": _f(inp["state_conv"][0][sb].reshape(NSO * 2, F2)),
        })
        in_maps.append(m)
    return run_bass_kernel_spmd(nc, in_maps, core_ids=list(range(ncores))).results


def kernel(**inputs):
    inp = {k: np.asarray(v) for k, v in inputs.items()}
    B, T, _ = inp["x_prompt"].shape
    NS = inp["x_sample"].shape[0]
    rs = run_samp(inp)
    att_all = np.concatenate([r["att_s"] for r in rs], axis=1)
    k_s = np.stack([r["ks"] for r in rs], axis=1).reshape(1, NS, 1, 8, 64)
    v_s = np.stack([r["vs"] for r in rs], axis=1).reshape(1, NS, 1, 8, 64)
    lf_s = np.stack([r["lfs"][:, 0] for r in rs], axis=1).reshape(1, NS, 1, 8)
    res = run_main(inp, att_all)
    cat = lambda n: np.concatenate([r[n] for r in res], axis=0)
    f32 = lambda a: np.ascontiguousarray(a, dtype=np.float32)
    return (f32(cat("y").reshape(B, T, D)), f32(cat("ys").reshape(NS, 1, D)),
            f32(cat("k").reshape(1, B, T, 8, 64)), f32(cat("v").reshape(1, B, T, 8, 64)), f32(cat("lf").reshape(1, B, T, 8)),
            f32(cat("mk").reshape(1, B, NMEM, 4, 256)), f32(cat("mv").reshape(1, B, NMEM, 4, 256)), f32(cat("cv").reshape(1, B, 2, F2)),
            f32(k_s), f32(v_s), f32(lf_s), f32(cat("zs").reshape(1, NS, 1, 8, 64)), f32(cat("cvs").reshape(1, NS, 2, F2)))
```

```python
import contextlib
import numpy as np
import concourse.bass as bass
import concourse.mybir as mybir
from concourse.bass_utils import run_bass_kernel_spmd

F32 = mybir.dt.float32
BF16 = mybir.dt.bfloat16
I32 = mybir.dt.int32
ALU = mybir.AluOpType
AF = mybir.ActivationFunctionType
AX = mybir.AxisListType

D = 1024
NIN = 2568
DFF = 2816
F2 = 5632
NMEM = 256
EPS = 1e-6
GC0 = 1.5957691216057308
GC1 = 0.07135481627260025


class Buf:
    __slots__ = ("w", "r", "excl")

    def __init__(self, excl=False):
        self.w = {}
        self.r = {}
        self.excl = excl


class Eng:
    def __init__(self, name, eng, sem, is_pe=False):
        self.name = name
        self.eng = eng
        self.sem = sem
        self.cnt = 0
        self.waited = {}
        self.is_pe = is_pe

    def wait(self, key, sem, val, owner):
        if owner is self and self.is_pe:
            return
        if self.waited.get(key, 0) >= val:
            return
        self.eng.wait_ge(sem, val)
        self.waited[key] = val


class Ctx:
    def __init__(self, nc, es):
        self.nc = nc
        self.es = es
        self.cur = es
        self.nm = 0
        self.dsem = []
        self.dsem_i = 0

    def name(self, p):
        self.nm += 1
        return f"{p}{self.nm}"

    def sb(self, shape, dt, p="t"):
        return self.cur.enter_context(self.nc.sbuf_tensor(self.name(p), list(shape), dt))

    @contextlib.contextmanager
    def scope(self):
        prev = self.cur
        st = contextlib.ExitStack()
        self.cur = st
        try:
            yield
            self.full_barrier()
        finally:
            self.cur = prev
            st.close()

    def full_barrier(self):
        engs = (self.PE, self.ACT, self.DVE, self.POOL, self.SP)
        for E in engs:
            for ds in self.dsem:
                if ds[1] > 0:
                    E.wait(id(ds[0]), ds[0], ds[1], None)
            for O in engs:
                if O is not E and O.cnt > 0:
                    E.wait(id(O.sem), O.sem, O.cnt, O)

    def ps(self, shape, dt, p="p"):
        return self.es.enter_context(self.nc.psum_tensor(self.name(p), list(shape), dt))

    def sem(self, p="s"):
        return self.es.enter_context(self.nc.semaphore(self.name(p)))

    def setup(self):
        nc = self.nc
        self.PE = Eng("pe", nc.tensor, self.sem("pe"), True)
        self.ACT = Eng("act", nc.scalar, self.sem("act"))
        self.DVE = Eng("dve", nc.vector, self.sem("dve"))
        self.POOL = Eng("pool", nc.gpsimd, self.sem("pool"))
        self.SP = Eng("sp", nc.sync, self.sem("sp"))
        for i in range(40):
            self.dsem.append([self.sem("d"), 0])

    def _waits(self, E, reads, writes):
        for b in reads:
            for k, (s, v, o) in b.w.items():
                E.wait(k, s, v, o)
            if b.excl:
                for k, (s, v, o) in b.r.items():
                    if o is not E:
                        E.wait(k, s, v, o)
        for b in writes:
            for k, (s, v, o) in b.w.items():
                E.wait(k, s, v, o)
            for k, (s, v, o) in b.r.items():
                E.wait(k, s, v, o)

    def _mark(self, tok, reads, writes, merge=False):
        k = id(tok[0])
        for b in reads:
            b.r[k] = tok
        for b in writes:
            if merge:
                b.w[k] = tok
            else:
                b.w = {k: tok}
            b.r = {}

    def op(self, E, fn, reads=(), writes=()):
        self._waits(E, reads, writes)
        ins = fn(E.eng)
        E.cnt += 1
        ins.then_inc(E.sem, 1)
        self._mark((E.sem, E.cnt, E), reads, writes)

    def dma(self, Q, out, in_, reads=(), writes=(), indirect=None):
        self._waits(Q, reads, writes)
        ds = self.dsem[self.dsem_i]
        self.dsem_i = (self.dsem_i + 1) % len(self.dsem)
        Q.wait(id(ds[0]), ds[0], ds[1], None)
        if indirect is not None:
            ins = Q.eng.indirect_dma_start(out=out, out_offset=None, in_=in_,
                                           in_offset=bass.IndirectOffsetOnAxis(ap=indirect, axis=0))
        else:
            ins = Q.eng.dma_start(out=out, in_=in_)
        ds[1] += 16
        ins.then_inc(ds[0], 16)
        self._mark((ds[0], ds[1], None), reads, writes, merge=True)

    def finish(self):
        for ds in self.dsem:
            if ds[1] > 0:
                self.SP.wait(id(ds[0]), ds[0], ds[1], None)
        for E in (self.PE, self.ACT, self.DVE, self.POOL):
            if E.cnt > 0:
                self.SP.wait(id(E.sem), E.sem, E.cnt, E)


class Rot:
    def __init__(self, c, n, shape, dt, psum=False, p="r"):
        self.slots = []
        for i in range(n):
            t = c.ps(shape, dt, p) if psum else c.sb(shape, dt, p)
            self.slots.append((t, Buf(excl=psum)))
        self.i = 0

    def next(self):
        s = self.slots[self.i]
        self.i = (self.i + 1) % len(self.slots)
        return s


def build_main(NB, T, NSO):
    NT = T // 128
    NG = NT // 2
    nc = bass.Bass("TRN2", target_bir_lowering=False)

    def din(name, shape, dt=F32):
        return nc.dram_tensor(name, list(shape), dt, kind="ExternalInput").ap()

    def dout(name, shape, dt=F32):
        return nc.dram_tensor(name, list(shape), dt, kind="ExternalOutput").ap()

    def dscr(name, shape, dt=F32):
        return nc.dram_tensor(name, list(shape), dt, kind="Internal").ap()

    xp = din("xp", [NB * T, D])
    memp = din("memp", [NB * NMEM, D])
    w_in = din("w_in", [D, NIN])
    w_o = din("w_o", [D, D])
    w_cq = din("w_cq", [D, D])
    w_ck = din("w_ck", [D, D])
    w_cv = din("w_cv", [D, D])
    w_co = din("w_co", [D, D])
    w_up = din("w_up", [D, F2])
    w_down = din("w_down", [DFF, D])
    gvec = din("gvec", [6, D])
    g_sgu_v = din("g_sgu_v", [1, 512])
    b_f = din("b_f", [1, 8])
    w_sT = din("w_sT", [8, 128, 128])
    b_sT = din("b_sT", [128, 8])
    cwT = din("cwT", [128, 44, 4])
    ident_d = din("ident", [128, 128])
    mask_d = din("mask_le", [128, 128])
    xso = din("xso", [NSO, D])
    atto = din("atto", [NSO, 512])
    cmk = din("cmk", [NSO * NMEM, D])
    cmv = din("cmv", [NSO * NMEM, D])
    stc = din("stc", [NSO * 2, F2])
    sel4 = din("sel4", [NSO, NSO * 128])
    ws00x = din("ws00x", [1, 512])
    bs0x = din("bs0x", [1, 512])
    cwrow = din("cwrow", [4, F2])
    ys_o = dout("ys", [NSO, D])
    zs_o = dout("zs", [NSO, 512])
    cvs_o = dout("cvs", [NSO * 2, F2])
    x2s_s = dscr("x2s_s", [NSO, D])
    oc_s = dscr("oc_s", [NSO, D])

    y_o = dout("y", [NB * T, D])
    k_o = dout("k", [NB * T, 512])
    v_o = dout("v", [NB * T, 512])
    lf_o = dout("lf", [NB * T, 8])
    mk_o = dout("mk", [NB * NMEM, D])
    mv_o = dout("mv", [NB * NMEM, D])
    cv_o = dout("cv", [NB * 2, F2])

    qT_s = dscr("qT_s", [NB * 8 * 64, T], BF16)
    kT_s = dscr("kT_s", [NB * 8 * 64, T], BF16)
    va_s = dscr("va_s", [NB * T, 8 * 65], BF16)
    lf_s = dscr("lf_s", [NB * T, 8])
    sgu_s = dscr("sgu_s", [NB * T, 512], BF16)
    att_s = dscr("att_s2", [NB * T, 512])
    x2_s = dscr("x2_s", [NB * T, D])

    with contextlib.ExitStack() as es:
        c = Ctx(nc, es)
        c.setup()
        PE, ACT, DVE, POOL, SP = c.PE, c.ACT, c.DVE, c.POOL, c.SP
        block = es.enter_context(nc.Block())

        def const(shape, dt, src_ap, q=None):
            t = c.sb(shape, dt, "c")
            b = Buf()
            c.dma(q or SP, t[:], src_ap, writes=[b])
            return t, b

        ident_f, ident_fb = const([128, 128], F32, ident_d)
        mask_f, mask_fb = const([128, 128], F32, mask_d)
        ident = c.sb([128, 128], BF16, "c"); identb = Buf()
        c.op(DVE, lambda e: e.tensor_copy(out=ident[:], in_=ident_f[:]), [ident_fb], [identb])
        mask_b = c.sb([128, 128], BF16, "c"); mask_bb = Buf()
        c.op(DVE, lambda e: e.tensor_copy(out=mask_b[:], in_=mask_f[:]), [mask_fb], [mask_bb])
        ones_f = c.sb([128, 128], F32, "c"); ones_fb = Buf()
        c.op(DVE, lambda e: e.memset(ones_f[:], 1.0), [], [ones_fb])
        cw, cwb = const([128, 44, 4], F32, cwT)
        pA = Rot(c, 4, [128, 512], F32, psum=True, p="pa")
        pT = Rot(c, 2, [128, 1024], BF16, psum=True, p="pt")
        pS = Rot(c, 2, [128, 512], F32, psum=True, p="ps")

        class RotShared:
            def __init__(self, slots):
                self.slots = slots
                self.i = 0

            def next(self):
                sl = self.slots[self.i]
                self.i = (self.i + 1) % len(self.slots)
                return sl

        pX = RotShared(pA.slots + pS.slots)
        pmm = [pA]

        wst = {}
        wregb = Buf()

        def alloc_w(n):
            wst["w"] = c.sb([128, n], BF16, "w")

        def load_w(dst_off, w_ap, rows, cols, q=POOL):
            wreg = wst["w"]
            kcn = rows // 128
            view = wreg[:, dst_off:dst_off + kcn * cols].rearrange("p (k n) -> p k n", k=kcn)
            for k in range(kcn):
                for c0 in range(0, cols, 1024):
                    c1 = min(cols, c0 + 1024)
                    c.dma(q, view[:, k, c0:c1], w_ap[k * 128:(k + 1) * 128, c0:c1], writes=[wregb])
            return view

        junk = Rot(c, 2, [128, D], BF16, p="j")
        st1 = Rot(c, 6, [128, 1], F32, p="st")

        def rmsnorm(x_ap, xb, P, width, out_ap, outb, g_ap=None, gb=None):
            jt, jb = junk.next()
            ss, ssb = st1.next()
            c.op(DVE, lambda e: e.memset(ss[:P, :], 0.0), [], [ssb])
            c.op(ACT, lambda e: e.activation(out=jt[:P, :width], in_=x_ap, func=AF.Square, accum_out=ss[:P, :]),
                 [xb, ssb], [jb, ssb])
            rs, rsb = st1.next()
            c.op(ACT, lambda e: e.activation(out=rs[:P, :], in_=ss[:P, :], func=AF.Sqrt, scale=1.0 / width, bias=EPS),
                 [ssb], [rsb])
            rd, rdb = st1.next()
            c.op(DVE, lambda e: e.reciprocal(out=rd[:P, :], in_=rs[:P, :]), [rsb], [rdb])
            if g_ap is None:
                c.op(DVE, lambda e: e.tensor_scalar(out=out_ap, in0=x_ap, scalar1=rd[:P, :], scalar2=None, op0=ALU.mult),
                     [xb, rdb], [outb])
            else:
                c.op(DVE, lambda e: e.scalar_tensor_tensor(out=out_ap, in0=x_ap, scalar=rd[:P, :], in1=g_ap,
                                                           op0=ALU.mult, op1=ALU.mult),
                     [xb, rdb, gb], [outb])

        xT_rot = Rot(c, 2, [128, 8, 128], BF16, p="xT")

        def transpose_tile(src, srcb, P, nch):
            pt, ptb = pT.next()
            for ch in range(nch):
                c.op(PE, lambda e, ch=ch: e.transpose(out=pt[:, ch * 128:ch * 128 + P], in_=src[:P, ch * 128:(ch + 1) * 128],
                                                      identity=ident[:P, :P]),
                     [srcb, identb], [ptb])
            xt, xtb = xT_rot.next()
            c.op(DVE, lambda e: e.tensor_copy(out=xt[:, :nch, :P],
                                              in_=pt[:, :nch * 128].rearrange("p (c t) -> p c t", c=nch)[:, :, :P]),
                 [ptb], [xtb])
            return xt, xtb

        def mm_tok(xt, xtb, P, wview, nk, c0, ncols):
            ps, psb = pmm[0].next()
            for k in range(nk):
                c.op(PE, lambda e, k=k: e.matmul(ps[:P, :ncols], lhsT=xt[:, k, :P], rhs=wview[:, k, c0:c0 + ncols],
                                                 start=(k == 0), stop=(k == nk - 1)),
                     [xtb, wregb], [psb])
            return ps, psb

        def gelu_from_psum(ps, psb, P, n, out_ap, outb, ta, tab, tb_, tbb):
            c.op(ACT, lambda e: e.activation(out=ta[:P, :n], in_=ps[:P, :n], func=AF.Square), [psb], [tab])
            c.op(DVE, lambda e: e.tensor_scalar(out=ta[:P, :n], in0=ta[:P, :n], scalar1=GC1, scalar2=GC0,
                                                op0=ALU.mult, op1=ALU.add), [tab], [tab])
            c.op(DVE, lambda e: e.tensor_tensor(out=tb_[:P, :n], in0=ta[:P, :n], in1=ps[:P, :n], op=ALU.mult),
                 [tab, psb], [tbb])
            c.op(ACT, lambda e: e.activation(out=tb_[:P, :n], in_=tb_[:P, :n], func=AF.Sigmoid), [tbb], [tbb])
            c.op(DVE, lambda e: e.tensor_tensor(out=out_ap, in0=tb_[:P, :n], in1=ps[:P, :n], op=ALU.mult),
                 [tbb, psb], [outb])

        xin = Rot(c, 2, [128, D], F32, p="xin")
        xnb = Rot(c, 2, [128, D], BF16, p="xnb")
        rl = Rot(c, 4, [128, 1], F32, p="rl")
        with c.scope():
            grep, grepb = const([128, 6, D], F32, gvec.rearrange("(o g) d -> o g d", o=1).broadcast_to([128, 6, D]))
            gsv, gsvb = const([128, 512], F32, g_sgu_v.broadcast_to([128, 512]))
            bfr, bfrb = const([128, 8], F32, b_f.broadcast_to([128, 8]))
            bsT, bsTb = const([128, 8], F32, b_sT)
            wsT_f, wsT_fb = const([128, 8, 128], F32, w_sT.rearrange("h s t -> s h t"))
            wsT = c.sb([128, 8, 128], BF16, "c"); wsTb = Buf()
            for h in range(8):
                c.op(DVE, lambda e, h=h: e.tensor_tensor(out=wsT[:, h, :], in0=wsT_f[:, h, :], in1=mask_f[:], op=ALU.mult),
                     [wsT_fb, mask_fb], [wsTb])

            mkT = c.sb([128, NB, 8, NMEM], BF16, "mkT"); mkTb = Buf()
            mva = c.sb([128, NB, 2, 4 * 257], BF16, "mva"); mvab = Buf()
            c.op(DVE, lambda e: e.memset(mva[:], 1.0), [], [mvab])
            osb = Rot(c, 3, [128, 512], F32, p="osb")
            mgs = c.sb([NSO, D], BF16, "mgs"); mgsb = Buf()
            with c.scope():
                alloc_w(16384)
                wck = load_w(0, w_ck, D, D)
                wcv = load_w(8 * D, w_cv, D, D)
                mnT = c.sb([128, 8, NMEM], BF16, "mnT"); mnTb = Buf()
                for b in range(NB):
                    for mt in range(2):
                        r0 = b * NMEM + mt * 128
                        xt_, xb_ = xin.next()
                        c.dma(SP, xt_[:], memp[r0:r0 + 128, :], writes=[xb_])
                        xn_, xnb_ = xnb.next()
                        rmsnorm(xt_[:], xb_, 128, D, xn_[:], xnb_, grep[:, 2, :], grepb)
                        xT_, xTb_ = transpose_tile(xn_, xnb_, 128, 8)
                        c.op(DVE, lambda e, mt=mt: e.tensor_copy(out=mnT[:, :, mt * 128:(mt + 1) * 128], in_=xT_[:, :, :]),
                             [xTb_], [mnTb])
                        for (wv, dst, isv) in ((wck, mk_o, False), (wcv, mv_o, True)):
                            for cc in range(2):
                                ps, psb = mm_tok(xT_, xTb_, 128, wv, 8, cc * 512, 512)
                                o_, ob_ = osb.next()
                                c.op(ACT, lambda e: e.copy(out=o_[:], in_=ps[:, :]), [psb], [ob_])
                                c.dma(SP, dst[r0:r0 + 128, cc * 512:(cc + 1) * 512], o_[:], reads=[ob_])
                                if isv:
                                    for hh in range(2):
                                        m = cc * 2 + hh
                                        c.op(DVE, lambda e, m=m, hh=hh: e.tensor_copy(
                                            out=mva[:, b, mt, m * 257:m * 257 + 256], in_=o_[:, hh * 256:(hh + 1) * 256]),
                                            [ob_], [mvab])
                    for blk in range(8):
                        ps, psb = pA.next()
                        for k in range(8):
                            c.op(PE, lambda e, k=k, blk=blk: e.matmul(ps[:, :NMEM], lhsT=wck[:, k, blk * 128:(blk + 1) * 128],
                                                                      rhs=mnT[:, k, :], start=(k == 0), stop=(k == 7)),
                                 [wregb, mnTb], [psb])
                        c.op(ACT, lambda e, blk=blk: e.copy(out=mkT[:, b, blk, :], in_=ps[:, :NMEM]), [psb], [mkTb])

            with c.scope():
                alloc_w(20544)
                win = load_w(0, w_in, D, NIN)
                ga = Rot(c, 2, [128, 512], F32, p="ga")
                gb_ = Rot(c, 2, [128, 512], F32, p="gb")
                ubuf = Rot(c, 2, [128, 512], F32, p="u")
                zbuf = Rot(c, 2, [128, 512], F32, p="z")
                zbb = Rot(c, 2, [128, 512], BF16, p="zb")
                vab = Rot(c, 2, [128, 8, 65], BF16, p="va")
                for s_ in vab.slots:
                    c.op(DVE, lambda e, s_=s_: e.memset(s_[0][:], 1.0), [], [s_[1]])
                qkb = Rot(c, 3, [128, 128], BF16, p="qk")
                sm8 = Rot(c, 4, [128, 8], F32, p="s8")

                pmm[0] = pX
                def stage1_tile(src_ap, P, r0, prompt=True, b=0, tcol=0):
                    xt_, xb_ = xin.next()
                    c.dma(SP, xt_[:P, :], src_ap, writes=[xb_])
                    xn_, xnb_ = xnb.next()
                    rmsnorm(xt_[:P, :], xb_, P, D, xn_[:P, :], xnb_, grep[:P, 0, :], grepb)
                    xT_, xTb_ = transpose_tile(xn_, xnb_, P, 8)
                    for blk in range(8):
                        ps, psb = pX.next()
                        for k in range(8):
                            c.op(PE, lambda e, k=k, blk=blk: e.matmul(ps[:, :P], lhsT=win[:, k, blk * 128:(blk + 1) * 128],
                                                                      rhs=xT_[:, k, :P], start=(k == 0), stop=(k == 7)),
                                 [wregb, xTb_], [psb])
                        o_, ob_ = qkb.next()
                        sc = 0.125 if blk < 4 else 1.0
                        c.op(ACT, lambda e, sc=sc: e.activation(out=o_[:, :P], in_=ps[:, :P], func=AF.Copy, scale=sc), [psb], [ob_])
                        dst = qT_s if blk < 4 else kT_s
                        pr = blk % 4
                        c.dma(POOL, dst[b * 512 + pr * 128:b * 512 + (pr + 1) * 128, tcol:tcol + P], o_[:, :P], reads=[ob_])
                    for (c0, dsto, isv) in ((512, k_o, False), (1024, v_o, True)):
                        ps, psb = mm_tok(xT_, xTb_, P, win, 8, c0, 512)
                        o_, ob_ = osb.next()
                        c.op(ACT, lambda e: e.copy(out=o_[:P, :], in_=ps[:P, :]), [psb], [ob_])
                        c.dma(SP, dsto[r0:r0 + P, :], o_[:P, :], reads=[ob_])
                        if isv:
                            va_, vab_ = vab.next()
                            c.op(DVE, lambda e: e.tensor_copy(out=va_[:P, :, 0:64], in_=o_[:P, :].rearrange("p (h d) -> p h d", h=8)),
                                 [ob_], [vab_])
                            c.dma(POOL, va_s[r0:r0 + P, :], va_[:P, :, :].rearrange("p h d -> p (h d)"), reads=[vab_])
                    ps, psb = mm_tok(xT_, xTb_, P, win, 8, 1536, 8)
                    f1, f1b = sm8.next()
                    c.op(DVE, lambda e: e.tensor_tensor(out=f1[:P, :], in0=ps[:P, :8], in1=bfr[:P, :], op=ALU.add), [psb, bfrb], [f1b])
                    c.op(ACT, lambda e: e.activation(out=f1[:P, :], in_=f1[:P, :], func=AF.Exp, scale=-1.0), [f1b], [f1b])
                    c.op(ACT, lambda e: e.activation(out=f1[:P, :], in_=f1[:P, :], func=AF.Ln, bias=1.0), [f1b], [f1b])
                    f2, f2b = sm8.next()
                    c.op(DVE, lambda e: e.tensor_scalar(out=f2[:P, :], in0=f1[:P, :], scalar1=-1.0, scalar2=None, op0=ALU.mult), [f1b], [f2b])
                    c.dma(SP, lf_o[r0:r0 + P, :], f2[:P, :], reads=[f2b])
                    c.dma(SP, lf_s[r0:r0 + P, :], f2[:P, :], reads=[f2b])
                    ta, tab = ga.next(); tb_, tbb = gb_.next()
                    ps, psb = mm_tok(xT_, xTb_, P, win, 8, 1544, 512)
                    u_, ub_ = ubuf.next()
                    gelu_from_psum(ps, psb, P, 512, u_[:P, :], ub_, ta, tab, tb_, tbb)
                    ta, tab = ga.next(); tb_, tbb = gb_.next()
                    ps, psb = mm_tok(xT_, xTb_, P, win, 8, 2056, 512)
                    z_, zb_ = zbuf.next()
                    gelu_from_psum(ps, psb, P, 512, z_[:P, :], zb_, ta, tab, tb_, tbb)
                    zn_, znb_ = zbb.next()
                    rmsnorm(z_[:P, :], zb_, P, 512, zn_[:P, :], znb_, gsv[:P, :], gsvb)
                    ps, psb = pX.next()
                    for h in range(8):
                        c.op(PE, lambda e, h=h: e.matmul(ps[:P, h * 64:(h + 1) * 64], lhsT=wsT[:, h, :], rhs=zn_[:, h * 64:(h + 1) * 64],
                                                         start=True, stop=True), [wsTb, znb_], [psb])
                    mx, mxb = osb.next()
                    for h in range(8):
                        c.op(DVE, lambda e, h=h: e.scalar_tensor_tensor(out=mx[:P, h * 64:(h + 1) * 64], in0=ps[:P, h * 64:(h + 1) * 64],
                                                                        scalar=bsT[:P, h:h + 1], in1=u_[:P, h * 64:(h + 1) * 64],
                                                                        op0=ALU.add, op1=ALU.mult),
                             [psb, bsTb, ub_], [mxb])
                    sg_, sgb_ = zbb.next()
                    rmsnorm(mx[:P, :], mxb, P, 512, sg_[:P, :], sgb_, grep[:P, 5, 512:1024], grepb)
                    c.dma(POOL, sgu_s[r0:r0 + P, :], sg_[:P, :], reads=[sgb_])

                for b in range(NB):
                    for t in range(NT):
                        r0 = b * T + t * 128
                        stage1_tile(xp[r0:r0 + 128, :], 128, r0, True, b, t * 128)

                P = NSO
                xt_, xb_ = xin.next()
                c.dma(SP, xt_[:P, :], xso[:, :], writes=[xb_])
                xn_, xnb_ = xnb.next()
                rmsnorm(xt_[:P, :], xb_, P, D, xn_[:P, :], xnb_, grep[:P, 0, :], grepb)
                xT_, xTb_ = transpose_tile(xn_, xnb_, P, 8)
                ta, tab = ga.next(); tb_, tbb = gb_.next()
                ps, psb = mm_tok(xT_, xTb_, P, win, 8, 1544, 512)
                u_, ub_ = ubuf.next()
                gelu_from_psum(ps, psb, P, 512, u_[:P, :], ub_, ta, tab, tb_, tbb)
                ta, tab = ga.next(); tb_, tbb = gb_.next()
                ps, psb = mm_tok(xT_, xTb_, P, win, 8, 2056, 512)
                z_, zb_ = zbuf.next()
                gelu_from_psum(ps, psb, P, 512, z_[:P, :], zb_, ta, tab, tb_, tbb)
                zf, zfb = osb.next()
                rmsnorm(z_[:P, :], zb_, P, 512, zf[:P, :], zfb, gsv[:P, :], gsvb)
                c.dma(SP, zs_o[:, :], zf[:P, :], reads=[zfb])
                wsx, wsxb = const([NSO, 512], F32, ws00x.broadcast_to([NSO, 512]))
                bsx, bsxb = const([NSO, 512], F32, bs0x.broadcast_to([NSO, 512]))
                mx, mxb = osb.next()
                c.op(DVE, lambda e: e.tensor_tensor(out=mx[:P, :], in0=zf[:P, :], in1=wsx[:P, :], op=ALU.mult), [zfb, wsxb], [mxb])
                c.op(DVE, lambda e: e.tensor_tensor(out=mx[:P, :], in0=mx[:P, :], in1=bsx[:P, :], op=ALU.add), [mxb, bsxb], [mxb])
                c.op(DVE, lambda e: e.tensor_tensor(out=mx[:P, :], in0=mx[:P, :], in1=u_[:P, :], op=ALU.mult), [mxb, ub_], [mxb])
                rmsnorm(mx[:P, :], mxb, P, 512, mgs[:P, 512:1024], mgsb, grep[:P, 5, 512:1024], grepb)

                def barrier_all_dma(E):
                    for ds in c.dsem:
                        if ds[1] > 0:
                            E.wait(id(ds[0]), ds[0], ds[1], None)

                barrier_all_dma(SP)
                barrier_all_dma(POOL)

            with c.scope():
                kTh = Rot(c, 2, [64, T], BF16, p="kTh")
                qTh = Rot(c, 2, [64, T], BF16, p="qTh")
                vah = c.sb([128, NT, 8 * 65], BF16, "vah"); vahb = Buf()
                lfb = c.sb([128, NT * 8], F32, "lfb"); lfbb = Buf()
                Cc = c.sb([128, NT, 8], F32, "Cc"); Ccb = Buf()
                pre = c.sb([128, NT + 1, 8], F32, "pre"); preb = Buf()
                tbt = c.sb([128, NT, NG, 8], F32, "tbt"); tbtb = Buf()
                PTr = Rot(c, 4, [128, 256], BF16, p="PT")
                attsb = Rot(c, 2, [128, NT, 64], F32, p="att")
                for b in range(NB):
                    c.dma(SP, vah[:], va_s[b * T:(b + 1) * T, :].rearrange("(t s) c -> s t c", s=128), writes=[vahb])
                    c.dma(SP, lfb[:].rearrange("p (t h) -> p t h", h=8), lf_s[b * T:(b + 1) * T, :].rearrange("(t s) h -> s t h", s=128),
                          writes=[lfbb])
                    ps, psb = pA.next()
                    c.op(PE, lambda e: e.matmul(ps[:, :NT * 8], lhsT=mask_f[:], rhs=lfb[:], start=True, stop=True), [mask_fb, lfbb], [psb])
                    ps2, psb2 = pA.next()
                    c.op(PE, lambda e: e.matmul(ps2[:, :NT * 8], lhsT=ones_f[:], rhs=lfb[:], start=True, stop=True), [ones_fb, lfbb], [psb2])
                    c.op(DVE, lambda e: e.memset(pre[:, 0, :], 0.0), [], [preb])
                    for j in range(NT):
                        c.op(DVE, lambda e, j=j: e.tensor_tensor(out=pre[:, j + 1, :], in0=pre[:, j, :], in1=ps2[:, j * 8:(j + 1) * 8], op=ALU.add),
                             [preb, psb2], [preb])
                    c.op(DVE, lambda e: e.tensor_tensor(out=Cc[:], in0=ps[:, :NT * 8].rearrange("p (t h) -> p t h", h=8), in1=pre[:, 0:NT, :], op=ALU.add),
                         [psb, preb], [Ccb])
                    for g in range(NG):
                        for j in range(2 * g + 2):
                            c.op(DVE, lambda e, j=j, g=g: e.tensor_tensor(out=tbt[:, j, g, :], in0=pre[:, 2 * g + 1, :], in1=Cc[:, j, :], op=ALU.subtract),
                                 [preb, Ccb], [tbtb])
                    for h in range(8):
                        kt_, ktb_ = kTh.next()
                        qt_, qtb_ = qTh.next()
                        c.dma(SP, kt_[:], kT_s[b * 512 + h * 64:b * 512 + (h + 1) * 64, :], writes=[ktb_])
                        c.dma(SP, qt_[:], qT_s[b * 512 + h * 64:b * 512 + (h + 1) * 64, :], writes=[qtb_])
                        at_, atb_ = attsb.next()
                        accs = [pS.slots[0], pS.slots[1]]
                        items = [(g, j) for g in range(NG) for j in range(2 * g + 2)]
                        LA = 2
                        pend = []

                        def emit_qk(g, j):
                            lo = 1 if j == 2 * g + 1 else 0
                            ncol = (2 - lo) * 128
                            q0 = (2 * g + lo) * 128
                            ps, psb = pA.next()
                            c.op(PE, lambda e: e.matmul(ps[:, :ncol], lhsT=kt_[:, j * 128:(j + 1) * 128],
                                                        rhs=qt_[:, q0:q0 + ncol], start=True, stop=True),
                                 [ktb_, qtb_], [psb])
                            pt_, ptb_ = PTr.next()
                            c.op(ACT, lambda e: e.activation(out=pt_[:, :ncol], in_=ps[:, :ncol], func=AF.Exp,
                                                             bias=tbt[:, j, g, h:h + 1], scale=1.0),
                                 [psb, tbtb], [ptb_])
                            if j >= 2 * g:
                                c.op(DVE, lambda e: e.tensor_tensor(out=pt_[:, 0:128], in0=pt_[:, 0:128], in1=mask_b[:], op=ALU.mult),
                                     [ptb_, mask_bb], [ptb_])
                            return (g, j, lo, pt_, ptb_)

                        def emit_pv(g, j, lo, pt_, ptb_):
                            for ii in range(lo, 2):
                                i = 2 * g + ii
                                a_, ab_ = accs[ii]
                                cs = (ii - lo) * 128
                                c.op(PE, lambda e: e.matmul(a_[:, 0:65], lhsT=pt_[:, cs:cs + 128],
                                                            rhs=vah[:, j, h * 65:(h + 1) * 65],
                                                            start=(j == 0), stop=(j == i)),
                                     [ptb_, vahb], [ab_])
                            if j == 2 * g + 1:
                                for ii in range(2):
                                    i = 2 * g + ii
                                    a_, ab_ = accs[ii]
                                    r_, rb_ = rl.next()
                                    c.op(DVE, lambda e: e.reciprocal(out=r_[:], in_=a_[:, 64:65]), [ab_], [rb_])
                                    c.op(DVE, lambda e: e.tensor_scalar(out=at_[:, i, :], in0=a_[:, 0:64], scalar1=r_[:], scalar2=None, op0=ALU.mult),
                                         [ab_, rb_], [atb_])

                        for idx in range(len(items) + LA):
                            if idx < len(items):
                                pend.append(emit_qk(*items[idx]))
                            if idx >= LA:
                                emit_pv(*pend[idx - LA])
                        c.dma(POOL, att_s[b * T:(b + 1) * T, h * 64:(h + 1) * 64].rearrange("(t s) d -> s t d", s=128), at_[:], reads=[atb_])

                barrier_all_dma(SP)
                barrier_all_dma(POOL)

            with c.scope():
                alloc_w(24576)
                wo = load_w(0, w_o, D, D)
                wcq = load_w(8 * D, w_cq, D, D)
                wco = load_w(16 * D, w_co, D, D)
                attin = Rot(c, 1, [128, 512], F32, p="ai")
                mrg = Rot(c, 2, [128, D], BF16, p="mg")
                x1r = Rot(c, 4, [128, D], F32, p="x1")
                qcT = Rot(c, 3, [128, 8, 128], BF16, p="qcT")
                PcT = Rot(c, 2, [128, 2, 128], BF16, p="PcT")
                ocr = Rot(c, 2, [128, D], BF16, p="oc")
                x2r = Rot(c, 2, [128, D], F32, p="x2")
                def p1(b, t):
                    r0 = b * T + t * 128
                    ai, aib = attin.next()
                    c.dma(SP, ai[:], att_s[r0:r0 + 128, :], writes=[aib])
                    mg, mgb = mrg.next()
                    c.dma(SP, mg[:, 512:1024], sgu_s[r0:r0 + 128, :], writes=[mgb])
                    rmsnorm(ai[:], aib, 128, 512, mg[:, 0:512], mgb, grep[:, 5, 0:512], grepb)
                    mT, mTb = transpose_tile(mg, mgb, 128, 8)
                    xt_, xb_ = xin.next()
                    c.dma(SP, xt_[:], xp[r0:r0 + 128, :], writes=[xb_])
                    x1, x1b = x1r.next()
                    for cc in range(2):
                        ps, psb = mm_tok(mT, mTb, 128, wo, 8, cc * 512, 512)
                        c.op(DVE, lambda e, cc=cc: e.tensor_tensor(out=x1[:, cc * 512:(cc + 1) * 512], in0=ps[:, :], in1=xt_[:, cc * 512:(cc + 1) * 512], op=ALU.add),
                             [psb, xb_], [x1b])
                    xn_, xnb_ = xnb.next()
                    rmsnorm(x1[:], x1b, 128, D, xn_[:], xnb_, grep[:, 1, :], grepb)
                    return (b, r0, x1, x1b, xn_, xnb_)

                def p2(b, r0, x1, x1b, xn_, xnb_):
                    xT_, xTb_ = transpose_tile(xn_, xnb_, 128, 8)
                    qc, qcb = qcT.next()
                    for blk in range(8):
                        ps, psb = pX.next()
                        for k in range(8):
                            c.op(PE, lambda e, k=k, blk=blk: e.matmul(ps[:, :128], lhsT=wcq[:, k, blk * 128:(blk + 1) * 128], rhs=xT_[:, k, :],
                                                                      start=(k == 0), stop=(k == 7)), [wregb, xTb_], [psb])
                        c.op(ACT, lambda e, blk=blk: e.activation(out=qc[:, blk, :], in_=ps[:, :128], func=AF.Copy, scale=1.0 / 16.0), [psb], [qcb])
                    return (b, r0, x1, x1b, qc, qcb)

                def p3(b, r0, x1, x1b, qc, qcb):
                    oc, ocb = ocr.next()
                    for m in range(4):
                        pc, pcb = PcT.next()
                        for mb in range(2):
                            ps, psb = pX.next()
                            for cc in range(2):
                                c.op(PE, lambda e, cc=cc, mb=mb, m=m: e.matmul(ps[:, :128], lhsT=mkT[:, b, 2 * m + cc, mb * 128:(mb + 1) * 128],
                                                                               rhs=qc[:, 2 * m + cc, :], start=(cc == 0), stop=(cc == 1)),
                                     [mkTb, qcb], [psb])
                            c.op(ACT, lambda e, mb=mb: e.activation(out=pc[:, mb, :], in_=ps[:, :128], func=AF.Exp), [psb], [pcb])
                        ps, psb = pX.next()
                        for mb in range(2):
                            c.op(PE, lambda e, mb=mb, m=m: e.matmul(ps[:, :257], lhsT=pc[:, mb, :], rhs=mva[:, b, mb, m * 257:(m + 1) * 257],
                                                                    start=(mb == 0), stop=(mb == 1)), [pcb, mvab], [psb])
                        r_, rb_ = rl.next()
                        c.op(DVE, lambda e: e.reciprocal(out=r_[:], in_=ps[:, 256:257]), [psb], [rb_])
                        c.op(DVE, lambda e, m=m: e.tensor_scalar(out=oc[:, m * 256:(m + 1) * 256], in0=ps[:, 0:256], scalar1=r_[:], scalar2=None, op0=ALU.mult),
                             [psb, rb_], [ocb])
                    return (b, r0, x1, x1b, oc, ocb)

                def p4(b, r0, x1, x1b, oc, ocb):
                    oT, oTb = transpose_tile(oc, ocb, 128, 8)
                    x2, x2b = x2r.next()
                    for cc in range(2):
                        ps, psb = mm_tok(oT, oTb, 128, wco, 8, cc * 512, 512)
                        c.op(DVE, lambda e, cc=cc: e.tensor_tensor(out=x2[:, cc * 512:(cc + 1) * 512], in0=ps[:, :], in1=x1[:, cc * 512:(cc + 1) * 512], op=ALU.add),
                             [psb, x1b], [x2b])
                    c.dma(POOL, x2_s[r0:r0 + 128, :], x2[:], reads=[x2b])


                tiles34 = [(b, t) for b in range(NB) for t in range(NT)]
                s1, s2, s3 = [], [], []
                nt34 = len(tiles34)
                for n in range(nt34 + 3):
                    if n < nt34:
                        s1.append(p1(*tiles34[n]))
                    if 1 <= n <= nt34:
                        s2.append(p2(*s1[n - 1]))
                    if 2 <= n <= nt34 + 1:
                        s3.append(p3(*s2[n - 2]))
                    if n >= 3:
                        p4(*s3[n - 3])

                with c.scope():
                    P = NSO
                    ai, aib = attin.next()
                    c.dma(SP, ai[:P, :], atto[:, :], writes=[aib])
                    rmsnorm(ai[:P, :], aib, P, 512, mgs[:P, 0:512], mgsb, grep[:P, 5, 0:512], grepb)
                    mT, mTb = transpose_tile(mgs, mgsb, P, 8)
                    xt_, xb_ = xin.next()
                    c.dma(SP, xt_[:P, :], xso[:, :], writes=[xb_])
                    x1, x1b = x1r.next()
                    for cc in range(2):
                        ps, psb = mm_tok(mT, mTb, P, wo, 8, cc * 512, 512)
                        c.op(DVE, lambda e, cc=cc: e.tensor_tensor(out=x1[:P, cc * 512:(cc + 1) * 512], in0=ps[:P, :], in1=xt_[:P, cc * 512:(cc + 1) * 512], op=ALU.add),
                             [psb, xb_], [x1b])
                    xn_, xnb_ = xnb.next()
                    rmsnorm(x1[:P, :], x1b, P, D, xn_[:P, :], xnb_, grep[:P, 1, :], grepb)
                    xT_, xTb_ = transpose_tile(xn_, xnb_, P, 8)
                    qs = c.sb([NSO, D], F32, "qs"); qsb = Buf()
                    for cc in range(2):
                        ps, psb = mm_tok(xT_, xTb_, P, wcq, 8, cc * 512, 512)
                        c.op(ACT, lambda e, cc=cc: e.activation(out=qs[:P, cc * 512:(cc + 1) * 512], in_=ps[:P, :], func=AF.Copy, scale=1.0 / 16.0), [psb], [qsb])
                    sel_t, sel_b = const([NSO, NSO * 128], F32, sel4)
                    K2 = c.sb([128, 2, D], F32, "K2"); K2b = Buf()
                    V2 = c.sb([128, 2, D], F32, "V2"); V2b = Buf()
                    qrep = c.sb([128, D], F32, "qrep"); qrepb = Buf()
                    Sx = c.sb([128, 2, 4], F32, "Sx"); Sxb = Buf()
                    Pm = c.sb([128, 2, 4], F32, "Pm"); Pmb = Buf()
                    o4 = c.sb([4, D], F32, "o4"); o4b = Buf()
                    ocsb = Buf()
                    for b in range(NSO):
                        c.dma(SP, K2[:], cmk[b * NMEM:(b + 1) * NMEM, :].rearrange("(m p) d -> p m d", p=128), writes=[K2b])
                        c.dma(SP, V2[:], cmv[b * NMEM:(b + 1) * NMEM, :].rearrange("(m p) d -> p m d", p=128), writes=[V2b])
                        for cc in range(2):
                            ps, psb = pA.next()
                            c.op(PE, lambda e, cc=cc, b=b: e.matmul(ps[:, :512], lhsT=sel_t[:P, b * 128:(b + 1) * 128], rhs=qs[:P, cc * 512:(cc + 1) * 512],
                                                                    start=True, stop=True), [sel_b, qsb], [psb])
                            c.op(ACT, lambda e, cc=cc: e.copy(out=qrep[:, cc * 512:(cc + 1) * 512], in_=ps[:, :512]), [psb], [qrepb])
                        for mb in range(2):
                            c.op(DVE, lambda e, mb=mb: e.tensor_tensor(out=K2[:, mb, :], in0=K2[:, mb, :], in1=qrep[:], op=ALU.mult), [K2b, qrepb], [K2b])
                            c.op(DVE, lambda e, mb=mb: e.tensor_reduce(out=Sx[:, mb, :], in_=K2[:, mb, :].rearrange("p (m f) -> p m f", m=4), axis=AX.X, op=ALU.add),
                                 [K2b], [Sxb])
                        c.op(ACT, lambda e: e.activation(out=Pm[:].rearrange("p a b -> p (a b)"), in_=Sx[:].rearrange("p a b -> p (a b)"), func=AF.Exp), [Sxb], [Pmb])
                        pso = []
                        for cc in range(2):
                            ps, psb = pA.next()
                            for mb in range(2):
                                c.op(PE, lambda e, cc=cc, mb=mb: e.matmul(ps[:4, :512], lhsT=Pm[:, mb, :], rhs=V2[:, mb, cc * 512:(cc + 1) * 512],
                                                                          start=(mb == 0), stop=(mb == 1)), [Pmb, V2b], [psb])
                            pso.append((ps, psb))
                        psd, psdb = pS.next()
                        for mb in range(2):
                            c.op(PE, lambda e, mb=mb: e.matmul(psd[:4, 0:1], lhsT=Pm[:, mb, :], rhs=ones_f[:, 0:1], start=(mb == 0), stop=(mb == 1)),
                                 [Pmb, ones_fb], [psdb])
                        r_, rb_ = rl.next()
                        c.op(DVE, lambda e: e.reciprocal(out=r_[:4, :], in_=psd[:4, 0:1]), [psdb], [rb_])
                        for cc in range(2):
                            ps, psb = pso[cc]
                            c.op(DVE, lambda e, cc=cc, ps=ps: e.tensor_scalar(out=o4[:4, cc * 512:(cc + 1) * 512], in0=ps[:4, :512], scalar1=r_[:4, :], scalar2=None, op0=ALU.mult),
                                 [psb, rb_], [o4b])
                        for m in range(4):
                            c.dma(SP, oc_s[b:b + 1, m * 256:(m + 1) * 256], o4[m:m + 1, m * 256:(m + 1) * 256], reads=[o4b], writes=[ocsb])
                    ocf, ocfb = x2r.next()
                    c.dma(SP, ocf[:P, :], oc_s[:, :], reads=[ocsb], writes=[ocfb])
                    oc, ocb = ocr.next()
                    c.op(DVE, lambda e: e.tensor_copy(out=oc[:P, :], in_=ocf[:P, :]), [ocfb], [ocb])
                    oT, oTb = transpose_tile(oc, ocb, P, 8)
                    x2, x2b = x2r.next()
                    for cc in range(2):
                        ps, psb = mm_tok(oT, oTb, P, wco, 8, cc * 512, 512)
                        c.op(DVE, lambda e, cc=cc: e.tensor_tensor(out=x2[:P, cc * 512:(cc + 1) * 512], in0=ps[:P, :], in1=x1[:P, cc * 512:(cc + 1) * 512], op=ALU.add),
                             [psb, x1b], [x2b])
                    c.dma(POOL, x2s_s[:, :], x2[:P, :], reads=[x2b])

                barrier_all_dma(SP)
                barrier_all_dma(POOL)

        with c.scope():
            alloc_w(8 * F2 + 22 * D)
            pmm[0] = pA
            grep5, grep5b = const([128, 2, D], F32, gvec[3:5, :].rearrange("(o g) d -> o g d", o=1).broadcast_to([128, 2, D]))
            wup = load_w(0, w_up, D, F2)
            wdn = load_w(8 * F2, w_down, DFF, D)
            with c.scope():
                CH = 256 if T % 256 == 0 else 128
                TPC = CH // 128
                halos = [(c.sb([128, 44, 2], F32, "halo"), Buf()) for _ in range(2)]
                corr = c.sb([128, 44, 2], F32, "corr"); corrb = Buf()
                ctm = c.sb([128, 44, 2], F32, "ctm"); ctmb = Buf()
                xT2r = Rot(c, 2, [128, 8, CH], BF16, p="xT2")
                accr = Rot(c, 5, [128, 2, CH], F32, p="acc")
                sgr = Rot(c, 3, [128, CH], F32, p="sg")
                hT = c.sb([128, 22, CH], BF16, "hT"); hTb = Buf()
                cvsb = Rot(c, 1, [2, 512], F32, p="cv")
                for b in range(NB):
                    c.op(DVE, lambda e: e.memset(halos[0][0][:], 0.0), [], [halos[0][1]])
                    for ch in range(T // CH):
                        xT2, xT2b = xT2r.next()
                        halo, halob = halos[ch % 2]
                        halo_n, halo_nb = halos[(ch + 1) % 2]
                        c.op(POOL, lambda e: e.tensor_tensor(out=ctm[:, :, 0], in0=halo[:, :, 1], in1=cw[:, :, 1], op=ALU.mult), [halob, cwb], [ctmb])
                        c.op(POOL, lambda e: e.tensor_tensor(out=ctm[:, :, 1], in0=halo[:, :, 0], in1=cw[:, :, 0], op=ALU.mult), [halob, cwb, ctmb], [ctmb])
                        c.op(POOL, lambda e: e.tensor_tensor(out=corr[:, :, 0], in0=ctm[:, :, 0], in1=ctm[:, :, 1], op=ALU.add), [ctmb], [corrb])
                        c.op(POOL, lambda e: e.tensor_tensor(out=corr[:, :, 1], in0=halo[:, :, 1], in1=cw[:, :, 0], op=ALU.mult), [halob, cwb, corrb], [corrb])
                        xts = []
                        for ti in range(TPC):
                            r0 = b * T + ch * CH + ti * 128
                            x2, x2b = xin.next()
                            c.dma(SP, x2[:], x2_s[r0:r0 + 128, :], writes=[x2b])
                            xn_, xnb_ = xnb.next()
                            rmsnorm(x2[:], x2b, 128, D, xn_[:], xnb_, grep5[:, 0, :], grep5b)
                            pt, ptb = pT.next()
                            for k in range(8):
                                c.op(PE, lambda e, k=k: e.transpose(out=pt[:, k * 128:(k + 1) * 128], in_=xn_[:, k * 128:(k + 1) * 128], identity=ident[:]),
                                     [xnb_, identb], [ptb])
                            c.op(DVE, lambda e, ti=ti: e.tensor_copy(out=xT2[:, :, ti * 128:(ti + 1) * 128],
                                                                     in_=pt[:, :].rearrange("p (c t) -> p c t", c=8)), [ptb], [xT2b])
                            xts.append((x2, x2b, r0))
                        def front(i):
                            ps, psb = pA.next()
                            for hh, fb in enumerate((i, 22 + i)):
                                for k in range(8):
                                    c.op(PE, lambda e, k=k, fb=fb, hh=hh: e.matmul(ps[:, hh * CH:(hh + 1) * CH], lhsT=wup[:, k, fb * 128:(fb + 1) * 128],
                                                                                   rhs=xT2[:, k, :], start=(k == 0), stop=(k == 7)),
                                         [wregb, xT2b], [psb])
                            ac, acb = accr.next()
                            for hh, fb in enumerate((i, 22 + i)):
                                c.op(ACT, lambda e, hh=hh, fb=fb: e.activation(out=ac[:, hh, :], in_=ps[:, hh * CH:(hh + 1) * CH], func=AF.Identity,
                                                                               bias=cw[:, fb, 3:4], scale=cw[:, fb, 2:3]), [psb, cwb], [acb])
                            for hh, fb in enumerate((i, 22 + i)):
                                c.op(ACT, lambda e, hh=hh, fb=fb: e.copy(out=halo_n[:, fb, :], in_=ps[:, (hh + 1) * CH - 2:(hh + 1) * CH]), [psb], [halo_nb])
                            for hh, fb in enumerate((i, 22 + i)):
                                c.op(DVE, lambda e, hh=hh, fb=fb: e.scalar_tensor_tensor(out=ac[:, hh, 1:CH], in0=ps[:, hh * CH:(hh + 1) * CH - 1], scalar=cw[:, fb, 1:2],
                                                                                         in1=ac[:, hh, 1:CH], op0=ALU.mult, op1=ALU.add), [psb, cwb, acb], [acb])
                                c.op(DVE, lambda e, hh=hh, fb=fb: e.scalar_tensor_tensor(out=ac[:, hh, 2:CH], in0=ps[:, hh * CH:(hh + 1) * CH - 2], scalar=cw[:, fb, 0:1],
                                                                                         in1=ac[:, hh, 2:CH], op0=ALU.mult, op1=ALU.add), [psb, cwb, acb], [acb])
                            return (i, ac, acb)

                        def mid(i, ac, acb):
                            for hh, fb in enumerate((i, 22 + i)):
                                c.op(POOL, lambda e, hh=hh, fb=fb: e.tensor_tensor(out=ac[:, hh, 0:2], in0=ac[:, hh, 0:2], in1=corr[:, fb, :], op=ALU.add),
                                     [corrb, acb], [acb])
                            sg, sgb = sgr.next()
                            c.op(ACT, lambda e: e.activation(out=sg[:], in_=ac[:, 0, :], func=AF.Silu), [acb], [sgb])
                            return (i, ac, acb, sg, sgb)

                        def tail(i, ac, acb, sg, sgb):
                            c.op(POOL, lambda e: e.tensor_tensor(out=hT[:, i, :], in0=sg[:], in1=ac[:, 1, :], op=ALU.mult), [sgb, acb], [hTb])

                        q1 = []
                        q2 = []
                        for i in range(22 + 2):
                            if i < 22:
                                q1.append(front(i))
                            if 1 <= i <= 22:
                                q2.append(mid(*q1[i - 1]))
                            if i >= 2:
                                tail(*q2[i - 2])
                        for ti, (x2, x2b, r0) in enumerate(xts):
                            for cc in range(2):
                                ps, psb = pA.next()
                                for fb in range(22):
                                    c.op(PE, lambda e, fb=fb, cc=cc, ti=ti: e.matmul(ps[:, :], lhsT=hT[:, fb, ti * 128:(ti + 1) * 128], rhs=wdn[:, fb, cc * 512:(cc + 1) * 512],
                                                                                   start=(fb == 0), stop=(fb == 21)), [hTb, wregb], [psb])
                                c.op(DVE, lambda e, cc=cc, x2=x2: e.tensor_tensor(out=x2[:, cc * 512:(cc + 1) * 512], in0=ps[:, :], in1=x2[:, cc * 512:(cc + 1) * 512], op=ALU.add),
                                     [psb, x2b], [x2b])
                            rmsnorm(x2[:], x2b, 128, D, x2[:], x2b, grep5[:, 1, :], grep5b)
                            c.dma(POOL, y_o[r0:r0 + 128, :], x2[:], reads=[x2b])
                        if ch == T // CH - 1:
                            for cc in range(11):
                                ps, psb = pS.next()
                                for k in range(8):
                                    c.op(PE, lambda e, k=k, cc=cc: e.matmul(ps[:2, :], lhsT=xT2[:, k, CH - 2:CH], rhs=wup[:, k, cc * 512:(cc + 1) * 512],
                                                                            start=(k == 0), stop=(k == 7)), [xT2b, wregb], [psb])
                                cv, cvb = cvsb.next()
                                c.op(ACT, lambda e: e.copy(out=cv[:, :], in_=ps[:2, :]), [psb], [cvb])
                                c.dma(POOL, cv_o[b * 2:b * 2 + 2, cc * 512:(cc + 1) * 512], cv[:, :], reads=[cvb])

            with c.scope():
                P = NSO
                x2, x2b = xin.next()
                c.dma(SP, x2[:P, :], x2s_s[:, :], writes=[x2b])
                xn_, xnb_ = xnb.next()
                rmsnorm(x2[:P, :], x2b, P, D, xn_[:P, :], xnb_, grep5[:P, 0, :], grep5b)
                xT_, xTb_ = transpose_tile(xn_, xnb_, P, 8)
                hid = c.sb([NSO, DFF], BF16, "hid"); hidb = Buf()
                hpr = Rot(c, 2, [NSO, 512], F32, p="hp")
                pvr = Rot(c, 1, [NSO, 2, 512], F32, p="pv")
                cwpr = Rot(c, 1, [NSO, 4, 512], F32, p="cwp")
                accr = Rot(c, 1, [NSO, 512], F32, p="acc")
                tmpr = Rot(c, 1, [NSO, 512], F32, p="tmp")
                sgr = Rot(c, 1, [NSO, 256], F32, p="sg")
                stc3 = stc.rearrange("(b r) f -> b r f", r=2)
                cvs3 = cvs_o.rearrange("(b r) f -> b r f", r=2)
                for i in range(11):
                    g0 = i * 256
                    v0 = DFF + i * 256
                    ps, psb = pA.next()
                    for (d0, c0) in ((0, g0), (256, v0)):
                        for k in range(8):
                            c.op(PE, lambda e, k=k, d0=d0, c0=c0: e.matmul(ps[:P, d0:d0 + 256], lhsT=xT_[:, k, :P], rhs=wup[:, k, c0:c0 + 256],
                                                                           start=(k == 0), stop=(k == 7)), [xTb_, wregb], [psb])
                    h_, hb_ = hpr.next()
                    c.op(ACT, lambda e: e.copy(out=h_[:P, :], in_=ps[:P, :]), [psb], [hb_])
                    pv_, pvb_ = pvr.next()
                    c.dma(SP, pv_[:P, :, 0:256], stc3[:, :, g0:g0 + 256], writes=[pvb_])
                    c.dma(SP, pv_[:P, :, 256:512], stc3[:, :, v0:v0 + 256], writes=[pvb_])
                    cw_, cwb_ = cwpr.next()
                    c.dma(SP, cw_[:P, :, 0:256], cwrow[:, g0:g0 + 256].rearrange("(o r) f -> o r f", o=1).broadcast_to([P, 4, 256]), writes=[cwb_])
                    c.dma(SP, cw_[:P, :, 256:512], cwrow[:, v0:v0 + 256].rearrange("(o r) f -> o r f", o=1).broadcast_to([P, 4, 256]), writes=[cwb_])
                    c.dma(POOL, cvs3[:, 0, g0:g0 + 256], pv_[:P, 1, 0:256], reads=[pvb_])
                    c.dma(POOL, cvs3[:, 0, v0:v0 + 256], pv_[:P, 1, 256:512], reads=[pvb_])
                    c.dma(POOL, cvs3[:, 1, g0:g0 + 256], h_[:P, 0:256], reads=[hb_])
                    c.dma(POOL, cvs3[:, 1, v0:v0 + 256], h_[:P, 256:512], reads=[hb_])
                    a_, ab_ = accr.next()
                    t_, tb2 = tmpr.next()
                    c.op(DVE, lambda e: e.tensor_tensor(out=a_[:P, :], in0=h_[:P, :], in1=cw_[:P, 2, :], op=ALU.mult), [hb_, cwb_], [ab_])
                    c.op(DVE, lambda e: e.tensor_tensor(out=a_[:P, :], in0=a_[:P, :], in1=cw_[:P, 3, :], op=ALU.add), [ab_, cwb_], [ab_])
                    c.op(DVE, lambda e: e.tensor_tensor(out=t_[:P, :], in0=pv_[:P, 1, :], in1=cw_[:P, 1, :], op=ALU.mult), [pvb_, cwb_], [tb2])
                    c.op(DVE, lambda e: e.tensor_tensor(out=a_[:P, :], in0=a_[:P, :], in1=t_[:P, :], op=ALU.add), [ab_, tb2], [ab_])
                    c.op(DVE, lambda e: e.tensor_tensor(out=t_[:P, :], in0=pv_[:P, 0, :], in1=cw_[:P, 0, :], op=ALU.mult), [pvb_, cwb_], [tb2])
                    c.op(DVE, lambda e: e.tensor_tensor(out=a_[:P, :], in0=a_[:P, :], in1=t_[:P, :], op=ALU.add), [ab_, tb2], [ab_])
                    s_, sb_ = sgr.next()
                    c.op(ACT, lambda e: e.activation(out=s_[:P, :], in_=a_[:P, 0:256], func=AF.Silu), [ab_], [sb_])
                    c.op(DVE, lambda e, i=i: e.tensor_tensor(out=hid[:P, i * 256:(i + 1) * 256], in0=s_[:P, :], in1=a_[:P, 256:512], op=ALU.mult),
                         [sb_, ab_], [hidb])
                hTs = c.sb([128, 22, NSO], BF16, "hTs"); hTsb = Buf()
                for (c0, n) in ((0, 8), (8, 8), (16, 6)):
                    xt3, xt3b = transpose_tile(hid[:, c0 * 128:(c0 + n) * 128], hidb, P, n)
                    c.op(DVE, lambda e, c0=c0, n=n, xt3=xt3: e.tensor_copy(out=hTs[:, c0:c0 + n, :], in_=xt3[:, :n, :P]), [xt3b], [hTsb])
                y_, yb_ = xin.next()
                for cc in range(2):
                    ps, psb = pA.next()
                    for fb in range(22):
                        c.op(PE, lambda e, fb=fb, cc=cc: e.matmul(ps[:P, :], lhsT=hTs[:, fb, :], rhs=wdn[:, fb, cc * 512:(cc + 1) * 512],
                                                                  start=(fb == 0), stop=(fb == 21)), [hTsb, wregb], [psb])
                    c.op(DVE, lambda e, cc=cc: e.tensor_tensor(out=y_[:P, cc * 512:(cc + 1) * 512], in0=ps[:P, :], in1=x2[:P, cc * 512:(cc + 1) * 512], op=ALU.add),
                         [psb, x2b], [yb_])
                rmsnorm(y_[:P, :], yb_, P, D, y_[:P, :], yb_, grep5[:P, 1, :], grep5b)
                c.dma(POOL, ys_o[:, :], y_[:P, :], reads=[yb_])

        c.finish()
    return nc


def build_samp(NS, NPG, NPHYS):
    nc = bass.Bass("TRN2", target_bir_lowering=False)

    def din(name, shape, dt=F32):
        return nc.dram_tensor(name, list(shape), dt, kind="ExternalInput").ap()

    def dout(name, shape, dt=F32):
        return nc.dram_tensor(name, list(shape), dt, kind="ExternalOutput").ap()

    xs = din("xs", [NS, D])
    w_in_c = din("w_in_c", [D, 193])
    b_f_c = din("b_f_c", [1, 1])
    g_mix = din("g_mix", [1, D])
    kc = din("kc", [NPHYS, 128 * 64])
    vc = din("vc", [NPHYS, 128 * 64])
    lfc = din("lfc", [NPHYS, 128])
    ptT = din("ptT", [NPG, NS], I32)
    ident_d = din("ident", [128, 128])
    sel_d = din("sel", [NS, NS * 128])
    mgt_d = din("mask_gt", [128, 128])
    att_o = dout("att_s", [NS, 64])
    ks_o = dout("ks", [NS, 64])
    vs_o = dout("vs", [NS, 64])
    lfs_o = dout("lfs", [NS, 1])
    res_s = nc.dram_tensor("res_s", [NS, 65], F32, kind="Internal").ap()
    PG = NPG

    with contextlib.ExitStack() as es:
        c = Ctx(nc, es)
        c.setup()
        PE, ACT, DVE, POOL, SP = c.PE, c.ACT, c.DVE, c.POOL, c.SP
        es.enter_context(nc.Block())

        def const(shape, dt, src_ap, q=None):
            t = c.sb(shape, dt, "c")
            b = Buf()
            c.dma(q or SP, t[:], src_ap, writes=[b])
            return t, b

        ident_f, ident_fb = const([128, 128], F32, ident_d)
        ident = c.sb([128, 128], BF16, "c"); identb = Buf()
        c.op(DVE, lambda e: e.tensor_copy(out=ident[:], in_=ident_f[:]), [ident_fb], [identb])
        sel_t, sel_b = const([NS, NS * 128], F32, sel_d)
        mgt, mgtb = const([128, 128], F32, mgt_d)
        ones_f = c.sb([128, 1], F32, "c"); ones_fb = Buf()
        c.op(DVE, lambda e: e.memset(ones_f[:], 1.0), [], [ones_fb])
        gm, gmb = const([NS, D], F32, g_mix.broadcast_to([NS, D]))
        bfc, bfcb = const([NS, 1], F32, b_f_c.broadcast_to([NS, 1]))
        pt_sb, pt_sbb = const([PG, NS], I32, ptT)
        wc = c.sb([128, 8, 193], BF16, "wc"); wcb = Buf()
        for k in range(8):
            c.dma(POOL, wc[:, k, :], w_in_c[k * 128:(k + 1) * 128, :], writes=[wcb])
        pA = Rot(c, 3, [128, 512], F32, psum=True, p="pa")
        pT = Rot(c, 1, [128, 1024], BF16, psum=True, p="pt")

        x_t = c.sb([NS, D], F32, "x"); xb = Buf()
        c.dma(SP, x_t[:], xs, writes=[xb])
        jt = c.sb([NS, D], BF16, "j"); jb = Buf()
        ss = c.sb([NS, 1], F32, "ss"); ssb = Buf()
        c.op(DVE, lambda e: e.memset(ss[:], 0.0), [], [ssb])
        c.op(ACT, lambda e: e.activation(out=jt[:], in_=x_t[:], func=AF.Square, accum_out=ss[:]), [xb, ssb], [jb, ssb])
        rs = c.sb([NS, 1], F32, "rs"); rsb = Buf()
        c.op(ACT, lambda e: e.activation(out=rs[:], in_=ss[:], func=AF.Sqrt, scale=1.0 / D, bias=EPS), [ssb], [rsb])
        rd = c.sb([NS, 1], F32, "rd"); rdb = Buf()
        c.op(DVE, lambda e: e.reciprocal(out=rd[:], in_=rs[:]), [rsb], [rdb])
        xn = c.sb([NS, D], BF16, "xn"); xnb_ = Buf()
        c.op(DVE, lambda e: e.scalar_tensor_tensor(out=xn[:], in0=x_t[:], scalar=rd[:], in1=gm[:], op0=ALU.mult, op1=ALU.mult),
             [xb, rdb, gmb], [xnb_])
        pt, ptb = pT.next()
        for ch in range(8):
            c.op(PE, lambda e, ch=ch: e.transpose(out=pt[:, ch * 128:ch * 128 + NS], in_=xn[:NS, ch * 128:(ch + 1) * 128], identity=ident[:NS, :NS]),
                 [xnb_, identb], [ptb])
        xT = c.sb([128, 8, NS], BF16, "xT"); xTb = Buf()
        c.op(DVE, lambda e: e.tensor_copy(out=xT[:], in_=pt[:, :].rearrange("p (c t) -> p c t", c=8)[:, :, :NS]), [ptb], [xTb])
        ps, psb = pA.next()
        for k in range(8):
            c.op(PE, lambda e, k=k: e.matmul(ps[:NS, :193], lhsT=xT[:, k, :], rhs=wc[:, k, :], start=(k == 0), stop=(k == 7)), [xTb, wcb], [psb])
        qx = c.sb([NS, 65], F32, "qx"); qxb = Buf()
        ksb = c.sb([NS, 64], F32, "ks"); ksbb = Buf()
        vsb = c.sb([NS, 64], F32, "vs"); vsbb = Buf()
        c.op(ACT, lambda e: e.activation(out=qx[:, 0:64], in_=ps[:NS, 0:64], func=AF.Copy, scale=0.125), [psb], [qxb])
        c.op(ACT, lambda e: e.copy(out=ksb[:], in_=ps[:NS, 64:128]), [psb], [ksbb])
        c.op(ACT, lambda e: e.copy(out=vsb[:], in_=ps[:NS, 128:192]), [psb], [vsbb])
        f1 = c.sb([NS, 1], F32, "f1"); f1b = Buf()
        c.op(DVE, lambda e: e.tensor_tensor(out=f1[:], in0=ps[:NS, 192:193], in1=bfc[:], op=ALU.add), [psb, bfcb], [f1b])
        c.op(ACT, lambda e: e.activation(out=f1[:], in_=f1[:], func=AF.Exp, scale=-1.0), [f1b], [f1b])
        c.op(ACT, lambda e: e.activation(out=f1[:], in_=f1[:], func=AF.Ln, bias=1.0), [f1b], [f1b])
        c.op(DVE, lambda e: e.tensor_scalar(out=qx[:, 64:65], in0=f1[:], scalar1=-1.0, scalar2=None, op0=ALU.mult), [f1b, qxb], [qxb])
        c.dma(SP, ks_o, ksb[:], reads=[ksbb])
        c.dma(SP, vs_o, vsb[:], reads=[vsbb])
        c.dma(SP, lfs_o, qx[:, 64:65], reads=[qxb])

        Kr = Rot(c, 2, [128, 8192], F32, p="K")
        Vr = Rot(c, 2, [128, 8192], F32, p="V")
        LFr = Rot(c, 2, [128, 128], F32, p="LF")
        cA = c.sb([128, 128], F32, "cA"); cAb = Buf()
        cB = c.sb([128, 128], F32, "cB"); cBb = Buf()
        qrep = Rot(c, 2, [128, 65], F32, p="qrep")
        sc = Rot(c, 2, [128, 128], F32, p="sc")
        pp = Rot(c, 2, [128, 128], F32, p="pp")
        base = Rot(c, 2, [128, 1], F32, p="base")
        red = Rot(c, 2, [128, 65], F32, p="red")
        resrow = c.sb([1, NS * 65], F32, "resrow"); resrowb = Buf()
        import os
        DBG = int(os.environ.get('SAMP_DBG', '0'))
        for b in range(NS if DBG != 1 else 0):
            K_, Kb_ = Kr.next()
            V_, Vb_ = Vr.next()
            L_, Lb_ = LFr.next()
            idx = pt_sb[:PG, b:b + 1]
            c.dma(POOL, K_[:PG, :], kc[:, :], reads=[pt_sbb], writes=[Kb_], indirect=idx)
            c.dma(POOL, V_[:PG, :], vc[:, :], reads=[pt_sbb], writes=[Vb_], indirect=idx)
            c.dma(POOL, L_[:PG, :], lfc[:, :], reads=[pt_sbb], writes=[Lb_], indirect=idx)
            if DBG == 2:
                continue
            ps, psb = pA.next()
            c.op(PE, lambda e, b=b: e.matmul(ps[:PG, :65], lhsT=sel_t[:, b * 128:b * 128 + PG], rhs=qx[:, :], start=True, stop=True), [sel_b, qxb], [psb])
            qr, qrb = qrep.next()
            c.op(ACT, lambda e: e.copy(out=qr[:PG, :], in_=ps[:PG, :65]), [psb], [qrb])
            K3 = K_[:PG, :].rearrange("p (s d) -> p s d", d=64)
            c.op(DVE, lambda e: e.tensor_tensor(out=K3, in0=K3, in1=qr[:PG, 0:64].unsqueeze(1).broadcast_to([PG, 128, 64]), op=ALU.mult),
                 [Kb_, qrb], [Kb_])
            if DBG == 3:
                continue
            s_, sb_ = sc.next()
            c.op(DVE, lambda e: e.tensor_reduce(out=s_[:PG, :], in_=K3, axis=AX.X, op=ALU.add), [Kb_], [sb_])
            if DBG == 4:
                continue
            src, srcb = L_, Lb_
            dsts = [(cA, cAb), (cB, cBb)]
            di = 0
            k = 1
            while k < 128:
                dst, dstb = dsts[di]
                di ^= 1
                c.op(DVE, lambda e, k=k, dst=dst, src=src: e.tensor_copy(out=dst[:PG, 0:k], in_=src[:PG, 0:k]), [srcb], [dstb])
                c.op(DVE, lambda e, k=k, dst=dst, src=src: e.tensor_tensor(out=dst[:PG, k:128], in0=src[:PG, k:128], in1=src[:PG, 0:128 - k], op=ALU.add),
                     [srcb, dstb], [dstb])
                src, srcb = dst, dstb
                k *= 2
            cin, cinb = src, srcb
            ps2, psb2 = pA.next()
            c.op(PE, lambda e: e.matmul(ps2[:PG, 0:1], lhsT=mgt[:PG, :PG], rhs=cin[:PG, 127:128], start=True, stop=True), [mgtb, cinb], [psb2])
            bs_, bsb_ = base.next()
            c.op(DVE, lambda e: e.tensor_tensor(out=bs_[:PG, :], in0=ps2[:PG, 0:1], in1=cin[:PG, 127:128], op=ALU.add), [psb2, cinb], [bsb_])
            c.op(DVE, lambda e: e.tensor_tensor(out=bs_[:PG, :], in0=bs_[:PG, :], in1=qr[:PG, 64:65], op=ALU.add), [bsb_, qrb], [bsb_])
            c.op(DVE, lambda e: e.tensor_tensor(out=s_[:PG, :], in0=s_[:PG, :], in1=cin[:PG, :], op=ALU.subtract), [sb_, cinb], [sb_])
            r_, rb_ = red.next()
            c.op(DVE, lambda e: e.memset(r_[:PG, 64:65], 0.0), [], [rb_])
            p_, pb_ = pp.next()
            c.op(ACT, lambda e: e.activation(out=p_[:PG, :], in_=s_[:PG, :], func=AF.Exp, bias=bs_[:PG, 0:1], scale=1.0, accum_out=r_[:PG, 64:65]),
                 [sb_, bsb_, rb_], [pb_, rb_])
            if DBG == 5:
                continue
            V3 = V_[:PG, :].rearrange("p (s d) -> p s d", d=64)
            c.op(DVE, lambda e: e.tensor_tensor(out=V3, in0=V3, in1=p_[:PG, :].unsqueeze(2).broadcast_to([PG, 128, 64]), op=ALU.mult),
                 [Vb_, pb_], [Vb_])
            c.op(DVE, lambda e: e.tensor_reduce(out=r_[:PG, 0:64], in_=V_[:PG, :].rearrange("p (s d) -> p d s", d=64), axis=AX.X, op=ALU.add),
                 [Vb_, rb_], [rb_])
            ps3, psb3 = pA.next()
            c.op(PE, lambda e: e.matmul(ps3[:1, :65], lhsT=ones_f[:PG, 0:1], rhs=r_[:PG, :], start=True, stop=True), [ones_fb, rb_], [psb3])
            c.op(ACT, lambda e, b=b: e.copy(out=resrow[0:1, b * 65:(b + 1) * 65], in_=ps3[:1, :65]), [psb3], [resrowb])
        ressb = Buf()
        c.dma(SP, res_s.rearrange("(o b) c -> o (b c)", o=1), resrow[:], reads=[resrowb], writes=[ressb])
        res = c.sb([NS, 65], F32, "res"); resb = Buf()
        c.dma(SP, res[:], res_s, reads=[ressb], writes=[resb])
        pr = c.sb([NS, 64], F32, "pr"); prb = Buf()
        c.op(DVE, lambda e: e.tensor_tensor(out=pr[:], in0=qx[:, 0:64], in1=ksb[:], op=ALU.mult), [qxb, ksbb], [prb])
        sn = c.sb([NS, 1], F32, "sn"); snb = Buf()
        c.op(DVE, lambda e: e.tensor_reduce(out=sn[:], in_=pr[:], axis=AX.X, op=ALU.add), [prb], [snb])
        c.op(ACT, lambda e: e.activation(out=sn[:], in_=sn[:], func=AF.Exp), [snb], [snb])
        num = c.sb([NS, 64], F32, "num"); numb = Buf()
        c.op(DVE, lambda e: e.scalar_tensor_tensor(out=num[:], in0=vsb[:], scalar=sn[:], in1=res[:, 0:64], op0=ALU.mult, op1=ALU.add),
             [vsbb, snb, resb], [numb])
        den = c.sb([NS, 1], F32, "den"); denb = Buf()
        c.op(DVE, lambda e: e.tensor_tensor(out=den[:], in0=res[:, 64:65], in1=sn[:], op=ALU.add), [resb, snb], [denb])
        c.op(DVE, lambda e: e.reciprocal(out=den[:], in_=den[:]), [denb], [denb])
        c.op(DVE, lambda e: e.tensor_scalar(out=num[:], in0=num[:], scalar1=den[:], scalar2=None, op0=ALU.mult), [numb, denb], [numb])
        c.dma(SP, att_o, num[:], reads=[numb])
        c.finish()
    return nc


_CACHE = {}


def _consts():
    ident = np.eye(128, dtype=np.float32)
    ii = np.arange(128)
    mask_le = (ii[:, None] <= ii[None, :]).astype(np.float32)
    mask_gt = (ii[:, None] > ii[None, :]).astype(np.float32)
    return ident, mask_le, mask_gt


def _f(a):
    return np.ascontiguousarray(a, dtype=np.float32)


def _sel(n):
    sel = np.zeros((n, n, 128), np.float32)
    for b in range(n):
        sel[b, b, :] = 1.0
    return sel.reshape(n, n * 128)


def run_samp(inp, ncores=8):
    NS = inp["x_sample"].shape[0]
    NPG = inp["page_table"].shape[1]
    NPHYS = inp["cache_k"].shape[1]
    key = ("samp", NS, NPG, NPHYS)
    if key not in _CACHE:
        _CACHE[key] = build_samp(NS, NPG, NPHYS)
    nc = _CACHE[key]
    ident, mask_le, mask_gt = _consts()
    w_in = inp["w_in"][0]
    sel = _sel(NS)
    ptT = np.ascontiguousarray(inp["page_table"].T.astype(np.int32))
    xs = _f(inp["x_sample"][:, 0, :])
    in_maps = []
    for h in range(ncores):
        cols = np.concatenate([np.arange(h * 64, (h + 1) * 64), 512 + np.arange(h * 64, (h + 1) * 64),
                               1024 + np.arange(h * 64, (h + 1) * 64), np.array([1536 + h])])
        in_maps.append({
            "xs": xs, "w_in_c": _f(w_in[:, cols]), "b_f_c": _f(inp["b_f"][:, h:h + 1]), "g_mix": _f(inp["g_mix"]),
            "kc": _f(inp["cache_k"][0][:, :, h, :].reshape(NPHYS, 128 * 64)),
            "vc": _f(inp["cache_v"][0][:, :, h, :].reshape(NPHYS, 128 * 64)),
            "lfc": _f(inp["cache_logf"][0][:, :, h]),
            "ptT": ptT, "ident": ident, "sel": sel, "mask_gt": mask_gt,
        })
    return run_bass_kernel_spmd(nc, in_maps, core_ids=list(range(ncores))).results


def run_main(inp, att_all, ncores=8):
    B, T, _ = inp["x_prompt"].shape
    NB = B // ncores
    NS = inp["x_sample"].shape[0]
    NSO = NS // ncores
    key = ("main", NB, T, NSO)
    if key not in _CACHE:
        _CACHE[key] = build_main(NB, T, NSO)
    nc = _CACHE[key]
    ident, mask_le, mask_gt = _consts()
    gvec = np.stack([inp["g_mix"][0], inp["g_cross"][0], inp["g_mem"][0], inp["g_ffn"][0], inp["g_final"],
                     np.concatenate([inp["g_att_out"][0], inp["g_sgu_out"][0]])]).astype(np.float32)
    cw4 = np.concatenate([inp["conv_w"][0], inp["conv_b"]], axis=0)
    cwT = _f(cw4.reshape(4, 44, 128).transpose(2, 1, 0))
    ws00x = _f(np.repeat(inp["w_s"][0][:, 0, 0], 64)[None, :])
    bs0x = _f(np.repeat(inp["b_s"][0][:, 0], 64)[None, :])
    sel4 = _sel(NSO)
    shared = {
        "w_in": _f(inp["w_in"][0]), "w_o": _f(inp["w_o"][0]), "w_cq": _f(inp["w_cq"][0]), "w_ck": _f(inp["w_ck"][0]),
        "w_cv": _f(inp["w_cv"][0]), "w_co": _f(inp["w_co"][0]), "w_up": _f(inp["w_up"][0]), "w_down": _f(inp["w_down"][0]),
        "gvec": gvec, "g_sgu_v": _f(inp["g_sgu_v"]), "b_f": _f(inp["b_f"]),
        "w_sT": _f(inp["w_s"][0].transpose(0, 2, 1)), "b_sT": _f(inp["b_s"][0].T), "cwT": cwT,
        "ident": ident, "mask_le": mask_le, "sel4": sel4, "ws00x": ws00x, "bs0x": bs0x, "cwrow": _f(cw4),
    }
    in_maps = []
    for ci in range(ncores):
        m = dict(shared)
        sb = slice(ci * NSO, (ci + 1) * NSO)
        m.update({
            "xp": _f(inp["x_prompt"][ci * NB:(ci + 1) * NB].reshape(NB * T, D)),
            "memp": _f(inp["mem_prompt"][ci * NB:(ci + 1) * NB].reshape(NB * NMEM, D)),
            "xso": _f(inp["x_sample"][sb, 0, :]),
            "atto": _f(att_all[sb]),
            "cmk": _f(inp["cache_mem_k"][0][sb].reshape(NSO * NMEM, D)),
            "cmv": _f(inp["cache_mem_v"][0][sb].reshape(NSO * NMEM, D)),
            "stc": _f(inp["state_conv"][0][sb].reshape(NSO * 2, F2)),
        })
        in_maps.append(m)
    return run_bass_kernel_spmd(nc, in_maps, core_ids=list(range(ncores))).results


def kernel(**inputs):
    inp = {k: np.asarray(v) for k, v in inputs.items()}
    B, T, _ = inp["x_prompt"].shape
    NS = inp["x_sample"].shape[0]
    rs = run_samp(inp)
    att_all = np.concatenate([r["att_s"] for r in rs], axis=1)
    k_s = np.stack([r["ks"] for r in rs], axis=1).reshape(1, NS, 1, 8, 64)
    v_s = np.stack([r["vs"] for r in rs], axis=1).reshape(1, NS, 1, 8, 64)
    lf_s = np.stack([r["lfs"][:, 0] for r in rs], axis=1).reshape(1, NS, 1, 8)
    res = run_main(inp, att_all)
    cat = lambda n: np.concatenate([r[n] for r in res], axis=0)
    f32 = lambda a: np.ascontiguousarray(a, dtype=np.float32)
    return (f32(cat("y").reshape(B, T, D)), f32(cat("ys").reshape(NS, 1, D)),
            f32(cat("k").reshape(1, B, T, 8, 64)), f32(cat("v").reshape(1, B, T, 8, 64)), f32(cat("lf").reshape(1, B, T, 8)),
            f32(cat("mk").reshape(1, B, NMEM, 4, 256)), f32(cat("mv").reshape(1, B, NMEM, 4, 256)), f32(cat("cv").reshape(1, B, 2, F2)),
            f32(k_s), f32(v_s), f32(lf_s), f32(cat("zs").reshape(1, NS, 1, 8, 64)), f32(cat("cvs").reshape(1, NS, 2, F2)))
```
